# Optimizing a Trainium2 kernel written in Bass

```python
import jax
import jax.numpy as jnp
from jax import lax
import numpy as np

D_MODEL = 1024
BATCH = 16
SEQ = 256
DEPTH = 4
DEC_BATCH = 8
DEC_SEQ = 1024
PAST_LEN = 256

GRID_W = 64
HEAD_DIM = 64
NA_HEADS = 4
GQA_Q_HEADS = 8
GQA_KV_HEADS = 2
MLSTM_HEADS = 4
NA_WIN_ROWS = 8
NA_WIN_COLS = 16
Q_BLOCK = 128
MLSTM_CHUNK = 64
ROPE_BASE = 10000.0
EPS = 1e-6
NEG = -1e30
NA_W = NA_HEADS * HEAD_DIM
GQA_QW = GQA_Q_HEADS * HEAD_DIM
GQA_KW = GQA_KV_HEADS * HEAD_DIM
ML_W = MLSTM_HEADS * HEAD_DIM
N_GATES = 4 * MLSTM_HEADS
MIX_WIDTH = NA_W + GQA_QW + ML_W
IN_SIZES = (NA_W, NA_W, NA_W, GQA_QW, GQA_KW, GQA_KW, ML_W, ML_W, ML_W, ML_W, N_GATES)
IN_WIDTH = 3 * NA_W + GQA_QW + 2 * GQA_KW + 4 * ML_W + N_GATES
FF_HIDDEN = ((8 * D_MODEL + 3 * 256 - 1) // (3 * 256)) * 256

kernel_name = 'hybrid_na_gqa_mlstm_diffusion_step'


def rms_norm(x, g):
    xf = x.astype(jnp.float32)
    y = xf * lax.rsqrt(jnp.mean(xf * xf, axis=-1, keepdims=True) + EPS)
    return (y * g.astype(jnp.float32)).astype(x.dtype)


def modulate(h, shift, scale):
    return h * (1 + scale) + shift


def adaln(cvec, w, b):
    mod = jnp.einsum('nd,de->ne', jax.nn.silu(cvec), w) + b
    return jnp.split(mod[:, None, :], 6, axis=-1)


def split_heads(x, n_heads):
    B, T, _ = x.shape
    return x.reshape(B, T, n_heads, HEAD_DIM).transpose(0, 2, 1, 3)


def merge_heads(x):
    B, H, T, d = x.shape
    return x.transpose(0, 2, 1, 3).reshape(B, T, H * d)


def rope_2d(x):
    T = x.shape[2]
    half = HEAD_DIM // 2
    quarter = half // 2
    inv = 1.0 / (ROPE_BASE ** (jnp.arange(quarter, dtype=jnp.float32) / quarter))
    t = jnp.arange(T)
    row = (t // GRID_W).astype(jnp.float32)
    col = (t % GRID_W).astype(jnp.float32)

    def rot(xh, pos):
        ang = pos[:, None] * inv[None, :]
        cos, sin = jnp.cos(ang), jnp.sin(ang)
        x1 = xh[..., :quarter].astype(jnp.float32)
        x2 = xh[..., quarter:].astype(jnp.float32)
        return jnp.concatenate([x1 * cos - x2 * sin, x2 * cos + x1 * sin], axis=-1)

    out = jnp.concatenate([rot(x[..., :half], row), rot(x[..., half:], col)], axis=-1)
    return out.astype(x.dtype)


def block_attention(q, k, v):
    B, Hq, T, d = q.shape
    Hk = k.shape[1]
    G = Hq // Hk
    NB = T // Q_BLOCK
    qb = q.reshape(B, Hk, G, NB, Q_BLOCK, d).transpose(3, 0, 1, 2, 4, 5)
    scale = d ** -0.5

    def one_block(qi):
        s = jnp.einsum('bhgqd,bhkd->bhgqk', qi, k).astype(jnp.float32) * scale
        p = jax.nn.softmax(s, axis=-1).astype(v.dtype)
        return jnp.einsum('bhgqk,bhkd->bhgqd', p, v)

    o = lax.map(one_block, qb)
    return o.transpose(1, 2, 3, 0, 4, 5).reshape(B, Hq, T, d)


def na_geometry(rows):
    kr = min(NA_WIN_ROWS, rows)
    r = np.arange(rows)
    r0 = np.clip(r - kr // 2, 0, rows - kr)
    row_idx = r0[:, None] + np.arange(kr)[None, :]
    cq = np.arange(GRID_W)
    c0 = np.clip(cq - NA_WIN_COLS // 2, 0, GRID_W - NA_WIN_COLS)
    col_mask = (cq[None, :] >= c0[:, None]) & (cq[None, :] < c0[:, None] + NA_WIN_COLS)
    dr = row_idx - r[:, None] + NA_WIN_ROWS - 1
    dc = np.clip(cq[None, :] - cq[:, None], -(NA_WIN_COLS - 1), NA_WIN_COLS - 1) + NA_WIN_COLS - 1
    return row_idx, col_mask, dr, dc


def na_latent(q, k, v, kc, vc, bias_table):
    B, H, T, d = q.shape
    rows = T // GRID_W
    row_idx, col_mask, dr, dc = na_geometry(rows)
    kr = row_idx.shape[1]
    scale = d ** -0.5
    qg = q.reshape(B, H, rows, GRID_W, d)
    kg = k.reshape(B, H, rows, GRID_W, d)[:, :, row_idx]
    vg = v.reshape(B, H, rows, GRID_W, d)[:, :, row_idx]
    s_win = jnp.einsum('bhrqd,bhrkwd->bhrqkw', qg, kg).astype(jnp.float32) * scale
    bias = bias_table[:, dr[:, :, None, None], dc[None, None, :, :]].transpose(0, 1, 3, 2, 4)
    s_win = jnp.where(col_mask[:, None, :], s_win + bias.astype(jnp.float32), NEG)
    s_ctx = jnp.einsum('bhrqd,bhld->bhrql', qg, kc).astype(jnp.float32) * scale
    nw = kr * GRID_W
    s = jnp.concatenate([s_win.reshape(B, H, rows, GRID_W, nw), s_ctx], axis=-1)
    p = jax.nn.softmax(s, axis=-1).astype(v.dtype)
    p_win = p[..., :nw].reshape(B, H, rows, GRID_W, kr, GRID_W)
    o = (jnp.einsum('bhrqkw,bhrkwd->bhrqd', p_win, vg)
         + jnp.einsum('bhrql,bhld->bhrqd', p[..., nw:], vc))
    return o.reshape(B, H, T, d)


def mlstm_scan(q, k, v, log_i, log_f, C0, n0, m0):
    B, H, T, d = q.shape
    L = MLSTM_CHUNK
    NC = T // L
    to_chunks = lambda x: jnp.moveaxis(x.reshape(B, H, NC, L, *x.shape[3:]), 2, 0)
    causal = jnp.tril(jnp.ones((L, L), dtype=bool))

    def chunk_step(carry, inp):
        C, n, m = carry
        qc, kc, vc, ic, fc = inp
        b = jnp.cumsum(fc, axis=-1)
        logw = jnp.where(causal, b[..., :, None] - b[..., None, :] + ic[..., None, :], -jnp.inf)
        m_inter = b + m[..., None]
        m_t = jnp.maximum(jnp.max(logw, axis=-1), m_inter)
        w = jnp.exp(logw - m_t[..., None])
        s = jnp.einsum('bhtk,bhsk->bhts', qc, kc) * w
        decay = jnp.exp(m_inter - m_t)
        num = (jnp.einsum('bhts,bhsv->bhtv', s, vc)
               + decay[..., None] * jnp.einsum('bhvk,bhtk->bhtv', C, qc))
        den = jnp.sum(s, axis=-1) + decay * jnp.einsum('bhk,bhtk->bht', n, qc)
        h = num / jnp.maximum(jnp.abs(den), jnp.exp(-m_t))[..., None]
        b_end = b[..., -1]
        logw_end = b_end[..., None] - b + ic
        m_new = jnp.maximum(b_end + m, jnp.max(logw_end, axis=-1))
        w_end = jnp.exp(logw_end - m_new[..., None])
        carry_decay = jnp.exp(b_end + m - m_new)
        C_new = carry_decay[..., None, None] * C + jnp.einsum('bhs,bhsv,bhsk->bhvk', w_end, vc, kc)
        n_new = carry_decay[..., None] * n + jnp.einsum('bhs,bhsk->bhk', w_end, kc)
        return (C_new, n_new, m_new), h

    xs = (to_chunks(q), to_chunks(k), to_chunks(v), to_chunks(log_i), to_chunks(log_f))
    (C, n, m), hs = lax.scan(chunk_step, (C0, n0, m0), xs)
    h = jnp.moveaxis(hs, 0, 2).reshape(B, H, T, d)
    return h, (C, n, m)


def mlstm_bidir(q, k, v, gates, st_fwd, st_bwd):
    B, T, _ = gates.shape
    g = gates.astype(jnp.float32).reshape(B, T, 4, MLSTM_HEADS).transpose(2, 0, 3, 1)
    f32 = lambda st: tuple(s.astype(jnp.float32) for s in st)
    h_f, fin_f = mlstm_scan(q, k, v, g[0], jax.nn.log_sigmoid(g[1]), *f32(st_fwd))
    flip = lambda x: jnp.flip(x, axis=2)
    h_b, fin_b = mlstm_scan(flip(q), flip(k), flip(v), flip(g[2]), flip(jax.nn.log_sigmoid(g[3])), *f32(st_bwd))
    return h_f + flip(h_b), fin_f, fin_b


def mlstm_output(h, o_pre, gain):
    B, H, T, d = h.shape
    hf = h.transpose(0, 2, 1, 3)
    hn = hf * lax.rsqrt(jnp.mean(hf * hf, axis=-1, keepdims=True) + EPS) * gain.reshape(H, d).astype(jnp.float32)
    return (hn.reshape(B, T, H * d) * jax.nn.sigmoid(o_pre.astype(jnp.float32))).astype(o_pre.dtype)


def token_mixers(h, w_in_l, b_gates_l, w_out_l, g_qk_l, g_ml_l, na_bias_l, ctx):
    offs = [int(o) for o in np.cumsum(IN_SIZES)[:-1]]
    z = jnp.einsum('btd,de->bte', h, w_in_l)
    nq, nk, nv, gq, gk, gv, mq, mk, mv, mo, mg = jnp.split(z, offs, axis=-1)
    mg = mg + b_gates_l
    nq, nk, nv = split_heads(nq, NA_HEADS), split_heads(nk, NA_HEADS), split_heads(nv, NA_HEADS)
    gq = rms_norm(split_heads(gq, GQA_Q_HEADS), g_qk_l[0])
    gk = rms_norm(split_heads(gk, GQA_KV_HEADS), g_qk_l[1])
    gv = split_heads(gv, GQA_KV_HEADS)
    mq = split_heads(mq, MLSTM_HEADS).astype(jnp.float32)
    mk = split_heads(mk, MLSTM_HEADS).astype(jnp.float32) * (HEAD_DIM ** -0.5)
    mv = split_heads(mv, MLSTM_HEADS).astype(jnp.float32)
    B = h.shape[0]
    if ctx is None:
        o_na = block_attention(nq, nk, nv)
        o_gqa = block_attention(gq, gk, gv)
        zero = (jnp.zeros((B, MLSTM_HEADS, HEAD_DIM, HEAD_DIM), jnp.float32),
                jnp.zeros((B, MLSTM_HEADS, HEAD_DIM), jnp.float32),
                jnp.zeros((B, MLSTM_HEADS), jnp.float32))
        st_f0, st_b0 = zero, zero
    else:
        na_kv, gqa_kv, C, n, m = ctx
        o_na = na_latent(nq, nk, nv, na_kv[:, 0], na_kv[:, 1], na_bias_l)
        k_all = jnp.concatenate([rope_2d(gk), gqa_kv[:, 0].astype(gk.dtype)], axis=2)
        v_all = jnp.concatenate([gv, gqa_kv[:, 1].astype(gv.dtype)], axis=2)
        o_gqa = block_attention(rope_2d(gq), k_all, v_all)
        st_f0 = (C[:, 0], n[:, 0], m[:, 0])
        st_b0 = (C[:, 1], n[:, 1], m[:, 1])
    h_ml, fin_f, fin_b = mlstm_bidir(mq, mk, mv, mg, st_f0, st_b0)
    o_ml = mlstm_output(h_ml, mo, g_ml_l)
    mix = jnp.concatenate([merge_heads(o_na), merge_heads(o_gqa), o_ml], axis=-1)
    out = jnp.einsum('bte,ed->btd', mix, w_out_l)
    if ctx is None:
        new_ctx = (jnp.stack([nk, nv], axis=1), jnp.stack([gk, gv], axis=1),
                   jnp.stack([fin_f[0], fin_b[0]], axis=1), jnp.stack([fin_f[1], fin_b[1]], axis=1),
                   jnp.stack([fin_f[2], fin_b[2]], axis=1))
    else:
        new_ctx = None
    return out, new_ctx


def swiglu(h, w_gu, w_down):
    gate, up = jnp.split(jnp.einsum('btd,df->btf', h, w_gu), 2, axis=-1)
    return jnp.einsum('btf,fd->btd', jax.nn.silu(gate) * up, w_down)


def trunk_layer(x, mods, mixer_w, g_norm_l, w_gu_l, w_down_l, ctx):
    sh1, sc1, ga1, sh2, sc2, ga2 = mods
    h = modulate(rms_norm(x, g_norm_l[0]), sh1, sc1)
    mix, new_ctx = token_mixers(h, *mixer_w, ctx)
    x = x + ga1 * rms_norm(mix, g_norm_l[1])
    f = swiglu(modulate(rms_norm(x, g_norm_l[2]), sh2, sc2), w_gu_l, w_down_l)
    x = x + ga2 * rms_norm(f, g_norm_l[3])
    return x, new_ctx


def setup_inputs(seed: int = 0) -> dict:
    key = jax.random.key(seed)
    ks = jax.random.split(key, 20)
    nrm = lambda k, shape, s: jax.random.normal(k, shape, jnp.float32) * s
    gate_offset = jnp.array([0.0, 3.0, 0.0, 3.0], jnp.float32)[None, :, None]
    return {
        'x_prompt': nrm(ks[0], (BATCH, SEQ, D_MODEL), 1.0),
        'x_sample': nrm(ks[1], (DEC_BATCH, DEC_SEQ, D_MODEL), 1.0),
        'cache_na_kv': nrm(ks[2], (DEC_BATCH, DEPTH, 2, NA_HEADS, PAST_LEN, HEAD_DIM), 1.0),
        'cache_gqa_kv': nrm(ks[3], (DEC_BATCH, DEPTH, 2, GQA_KV_HEADS, PAST_LEN, HEAD_DIM), 1.0),
        'state_mlstm_C': nrm(ks[4], (DEC_BATCH, DEPTH, 2, MLSTM_HEADS, HEAD_DIM, HEAD_DIM), HEAD_DIM ** -0.5),
        'state_mlstm_n': nrm(ks[5], (DEC_BATCH, DEPTH, 2, MLSTM_HEADS, HEAD_DIM), 0.5),
        'state_mlstm_m': nrm(ks[6], (DEC_BATCH, DEPTH, 2, MLSTM_HEADS), 1.0),
        'c': nrm(ks[7], (DEC_BATCH, D_MODEL), 1.0),
        'c_ctx': nrm(ks[8], (D_MODEL,), 1.0),
        'w_in': nrm(ks[9], (DEPTH, D_MODEL, IN_WIDTH), D_MODEL ** -0.5),
        'b_gates': (nrm(ks[10], (DEPTH, 4, MLSTM_HEADS), 0.1) + gate_offset).reshape(DEPTH, N_GATES),
        'w_out': nrm(ks[11], (DEPTH, MIX_WIDTH, D_MODEL), MIX_WIDTH ** -0.5),
        'g_norm': 1.0 + nrm(ks[12], (DEPTH, 4, D_MODEL), 0.02),
        'g_qk': 1.0 + nrm(ks[13], (DEPTH, 2, HEAD_DIM), 0.02),
        'g_mlstm': 1.0 + nrm(ks[14], (DEPTH, ML_W), 0.02),
        'na_bias': nrm(ks[15], (DEPTH, NA_HEADS, 2 * NA_WIN_ROWS - 1, 2 * NA_WIN_COLS - 1), 0.02),
        'w_ada': nrm(ks[16], (DEPTH, D_MODEL, 6 * D_MODEL), 0.5 * D_MODEL ** -0.5),
        'b_ada': nrm(ks[17], (DEPTH, 6 * D_MODEL), 0.02),
        'w_gu': nrm(ks[18], (DEPTH, D_MODEL, 2 * FF_HIDDEN), D_MODEL ** -0.5),
        'w_down': nrm(ks[19], (DEPTH, FF_HIDDEN, D_MODEL), FF_HIDDEN ** -0.5),
    }


def reference(x_prompt, x_sample, cache_na_kv, cache_gqa_kv, state_mlstm_C, state_mlstm_n, state_mlstm_m,
              c, c_ctx, w_in, b_gates, w_out, g_norm, g_qk, g_mlstm, na_bias, w_ada, b_ada, w_gu, w_down):
    xp, xs = x_prompt, x_sample
    na_l, gqa_l, C_l, n_l, m_l = [], [], [], [], []
    for l in range(DEPTH):
        mixer_w = (w_in[l], b_gates[l], w_out[l], g_qk[l], g_mlstm[l], na_bias[l])
        xp, (na_kv, gqa_kv, Cs, ns, ms) = trunk_layer(
            xp, adaln(c_ctx[None, :], w_ada[l], b_ada[l]), mixer_w, g_norm[l], w_gu[l], w_down[l], None)
        na_l.append(na_kv)
        gqa_l.append(gqa_kv)
        C_l.append(Cs)
        n_l.append(ns)
        m_l.append(ms)
        ctx = (cache_na_kv[:, l], cache_gqa_kv[:, l], state_mlstm_C[:, l], state_mlstm_n[:, l], state_mlstm_m[:, l])
        xs, _ = trunk_layer(xs, adaln(c, w_ada[l], b_ada[l]), mixer_w, g_norm[l], w_gu[l], w_down[l], ctx)
    new_na_kv = jnp.stack(na_l, axis=1)
    new_gqa_kv = jnp.stack(gqa_l, axis=1)
    new_C = jnp.stack(C_l, axis=1)
    new_n = jnp.stack(n_l, axis=1)
    new_m = jnp.stack(m_l, axis=1)
    return (xp, xs, new_na_kv, new_gqa_kv, new_C, new_n, new_m)
```

```python
import numpy as np
from contextlib import ExitStack
import concourse.bass as bass
import concourse.mybir as mybir
from concourse.bass_utils import run_bass_kernel_spmd

F32 = mybir.dt.float32
BF16 = mybir.dt.bfloat16
ALU = mybir.AluOpType
AF = mybir.ActivationFunctionType
AX = mybir.AxisListType

D = 1024
DEPTH = 4
TS = 1024
TP = 512
EPS = 1e-6
NEG = -1e30
FF = 2816
IN_W = 2576
LV = 88
NV = 4 * LV + 16
CF_COS, CF_SIN, CF_ID, CF_AL, CF_BE, CF_PHI, CF_OMP, CF_I8, CF_ONE8, CF_ZERO, CF_EPS, NF = 0, 1024, 2048, 2176, 2177, 2178, 2179, 2180, 2188, 2316, 2317, 2318
CB_ID, CB_ONES, CB_BLK, CB_R, CB_MF, CB_MB, CB_KA, CB_QA, NB = 0, 128, 256, 384, 512, 640, 768, 1792, 2816


def _prod(s):
    r = 1
    for x in s:
        r *= int(x)
    return r


STRICT = True


class Prog:
    ENGS = ("pe", "act", "dve", "pool", "sp")
    NSLOT = 8

    def __init__(self, nc, es):
        self.nc, self.es = nc, es
        self.ops = {e: [] for e in self.ENGS}
        self.recs = {}
        self.fbytes = {}
        self.waited = {e: {} for e in self.ENGS}
        self.ndma = {"sp": 0, "pool": 0, "act": 0}

    def sb(self, name, shape, dt):
        t = self.es.enter_context(self.nc.sbuf_tensor(name, list(shape), dt))
        self.fbytes[name] = _prod(shape[1:]) * mybir.dt.size(dt)
        return t

    def region(self, ap):
        sp = str(ap.space)
        name = ap.tensor.name
        ds = mybir.dt.size(ap.dtype)
        pat = ap.ap
        off = int(ap.offset)
        if "PSUM" in sp:
            return (name, 0, 128, 0, 1 << 40)
        if "DRAM" in sp:
            hi = off + sum((c - 1) * abs(s) for s, c in pat) + 1
            return (name, 0, 1, off * ds, hi * ds)
        fb = self.fbytes[name]
        ob = off * ds
        p0, f0 = ob // fb, ob % fb
        pstep, pcnt = pat[0]
        assert pcnt == 1 or pstep * ds == fb, (name, pat, fb)
        ext = sum((c - 1) * abs(s) for s, c in pat[1:]) * ds + ds
        return (name, p0, p0 + pcnt, f0, f0 + ext)

    @staticmethod
    def _ov(a, b):
        return a[1] < b[2] and b[1] < a[2] and a[3] < b[4] and b[3] < a[4]

    @staticmethod
    def _inside(a, b):
        return a[1] >= b[1] and a[2] <= b[2] and a[3] >= b[3] and a[4] <= b[4]

    def op(self, eng, fn, w=(), r=(), dma=False):
        idx = len(self.ops[eng])
        deps = []
        if dma:
            d = self.ndma[eng]
            self.ndma[eng] += 1
            slot, use = d % self.NSLOT, d // self.NSLOT
            tok = ("d", eng, slot, use)
            if use > 0:
                deps.append((("d", eng, slot, use - 1), "raw"))
        else:
            tok = ("e", eng, idx)
        key = tok[:3] if dma else tok[:2]
        for ap in r:
            reg = self.region(ap)
            lst = self.recs.setdefault(reg[0], [])
            psum_ap = "PSUM" in str(ap.space)
            for rec in lst:
                if rec[0] == "w" and self._ov(rec[2], reg):
                    deps.append((rec[1], "raw"))
                elif psum_ap and rec[0] == "r" and rec[3] != key and self._ov(rec[2], reg):
                    deps.append((rec[1], "rar"))
            for rec in lst:
                if rec[0] == "r" and rec[3] == key and rec[2] == reg:
                    rec[1] = tok
                    break
            else:
                lst.append(["r", tok, reg, key])
        for ap in w:
            reg = self.region(ap)
            lst = self.recs.setdefault(reg[0], [])
            keep = []
            for rec in lst:
                if self._ov(rec[2], reg):
                    deps.append((rec[1], "waw" if rec[0] == "w" else "war"))
                    if self._inside(rec[2], reg):
                        continue
                keep.append(rec)
            keep.append(["w", tok, reg, key])
            self.recs[reg[0]] = keep
        waits = []
        wd = self.waited[eng]
        for t, hz in deps:
            if t == tok:
                continue
            if t[0] == "e":
                if t[1] == eng and not dma:
                    if eng == "pe" or (hz != "raw" and not STRICT):
                        continue
                k, v = ("e", t[1]), t[2]
            else:
                k, v = t[:3], t[3]
            if wd.get(k, -1) >= v:
                continue
            wd[k] = v
            waits.append(t)
            if t[0] == "e":
                self.ops[t[1]][t[2]]["inc"] = True
        self.ops[eng].append(dict(fn=fn, waits=waits, inc=False, dma=tok if dma else None))

    def emit(self):
        nc, es = self.nc, self.es
        sems = {e: es.enter_context(nc.semaphore("s_" + e)) for e in ("pe", "act", "dve", "pool")}
        dsem = {q: [es.enter_context(nc.semaphore("d_%s%d" % (q, i))) for i in range(self.NSLOT)]
                for q in ("sp", "pool", "act")}
        fin = []
        for q, n in self.ndma.items():
            for s in range(min(n, self.NSLOT)):
                uses = (n - 1 - s) // self.NSLOT
                fin.append(("d", q, s, uses))
        self.ops["sp"].append(dict(fn=None, waits=fin, inc=False, dma=None))
        for e in self.ENGS:
            c = 0
            for o in self.ops[e]:
                if o["inc"]:
                    c += 1
                o["cnt"] = c

        def val(t):
            if t[0] == "e":
                return sems[t[1]], self.ops[t[1]][t[2]]["cnt"]
            return dsem[t[1]][t[2]], 16 * (t[3] + 1)

        def run(name, e):
            for o in self.ops[name]:
                for t in o["waits"]:
                    s, v = val(t)
                    e.wait_ge(s, v)
                if o["fn"] is None:
                    continue
                ins = o["fn"](e)
                if o["dma"] is not None:
                    ins.then_inc(dsem[o["dma"][1]][o["dma"][2]], 16)
                elif o["inc"]:
                    ins.then_inc(sems[name], 1)

        with nc.Block() as block:
            @block.tensor
            def _(e):
                run("pe", e)

            @block.scalar
            def _(e):
                run("act", e)

            @block.vector
            def _(e):
                run("dve", e)

            @block.gpsimd
            def _(e):
                run("pool", e)

            @block.sync
            def _(e):
                run("sp", e)


class _Stop(Exception):
    pass


def build(n_layers=DEPTH, dbg=None, stop=None):
    nc = bass.Bass("TRN2", target_bir_lowering=False)
    es = ExitStack()
    P = Prog(nc, es)
    dbg = dbg or []

    def din(name, shape):
        return nc.dram_tensor(name, list(shape), F32, kind="ExternalInput").ap()

    def dout(name, shape):
        return nc.dram_tensor(name, list(shape), F32, kind="ExternalOutput").ap()

    xs_d, xp_d = din("xs", [D, TS]), din("xp", [D, TP])
    w_in_d, w_out_d = din("w_in", [DEPTH, D, IN_W]), din("w_out", [DEPTH, D, D])
    w_gu_d, w_down_d = din("w_gu", [DEPTH, D, 2 * FF]), din("w_down", [DEPTH, FF, D])
    w_ada_d = din("w_ada", [DEPTH, D, 6 * D])
    colv_d, cstf_d = din("colv", [128, NV]), din("cstf", [128, NF])
    cstb_d = nc.dram_tensor("cstb", [128, NB], BF16, kind="ExternalInput").ap()
    nabT_d = din("nabT", [DEPTH, 4, 64, 15, 64])
    na_kT_d, na_v_d = din("na_kT", [DEPTH, 2, 128, 256]), din("na_v", [DEPTH, 4, 256, 64])
    gq_kT_d, gq_v_d = din("gq_kT", [DEPTH, 2, 128, 256]), din("gq_v", [DEPTH, 2, 256, 64])
    ml_C_d = din("ml_C", [DEPTH, 8, 64, 65])
    ys_d, yp_d = dout("ys", [D, TS]), dout("yp", [D, TP])
    o_nak, o_nav = dout("o_nak", [DEPTH, 2, 128, TP]), dout("o_nav", [DEPTH, TP, 256])
    o_gk, o_gv = dout("o_gk", [DEPTH, 128, TP]), dout("o_gv", [DEPTH, TP, 128])
    o_C, o_m = dout("o_C", [DEPTH, 2, 8, 64, 65]), dout("o_m", [DEPTH, 2, 8])

    xTs = P.sb("xTs", [128, 8, TS], F32)
    xTp = P.sb("xTp", [128, 8, TP], F32)
    hT = P.sb("hT", [128, 8, TS], BF16)
    BIGN = 23552
    BIG = P.sb("BIG", [128, BIGN], BF16)
    outT0 = P.sb("outT0", [128, 8, 512], F32)
    outT1 = hT[:, :, :].bitcast(F32)
    wring = [P.sb("wr%d" % i, [128, 4096], BF16) for i in range(3)]
    cstf = P.sb("cstf_s", [128, NF], F32)
    cstb = P.sb("cstb_s", [128, NB], BF16)
    colv = P.sb("colv_s", [128, NV], F32)
    g1024 = P.sb("g1024", [128, DEPTH, 4, 8], F32)
    sqt = [P.sb("sq%d" % i, [128, 512], BF16) for i in range(2)]
    rstd = P.sb("rstd", [128, 512], F32)
    tmpf = [P.sb("tmpf%d" % i, [128, 512], F32) for i in range(2)]
    ptl = [P.sb("pt%d" % i, [128, 512], BF16) for i in range(4)]
    rec = P.sb("rec", [128, 512], F32)
    stg = [P.sb("stg%d" % i, [128, 512], F32) for i in range(2)]
    mods2 = [P.sb("mods%d" % i, [128, 48, 2], F32) for i in range(2)]
    dsc2 = [P.sb("dsc%d" % i, [128, 4, 8, 2], F32) for i in range(2)]
    scbf = P.sb("scbf", [128, 8, 2], BF16)
    strip = [P.sb("strip%d" % i, [128, 22, 64], BF16) for i in range(2)]
    nabst = P.sb("nabst", [128, 15, 64], F32)
    wg2 = [P.sb("wg%d" % i, [128, 8, 16], BF16) for i in range(2)]
    qn2 = [P.sb("qn%d" % i, [128, 512], BF16) for i in range(2)]
    gsm = P.sb("gsm", [8, 16, 8], F32)
    tots = P.sb("tots", [8, 2], F32)
    nbf2 = [P.sb("nbf%d" % i, [8, 1], F32) for i in range(2)]
    rhsexp = P.sb("rhsexp", [8, 8, 8], F32)
    tokm = P.sb("tokm", [128, 3, 8, 8], F32)
    carry_bc = P.sb("carry_bc", [128, 8, 8], F32)
    C32 = P.sb("C32", [128, 8, 65], F32)
    C16 = P.sb("C16", [128, 8, 66], BF16)
    smt = [P.sb("smt%d" % i, [128, 128], BF16) for i in range(8)]
    uvt = [P.sb("uvt%d" % i, [128, 66], BF16) for i in range(8)]
    den = P.sb("den", [128, 4], F32)
    tmpH = P.sb("tmpH", [128, 256], F32)
    hnb2 = [P.sb("hnb%d" % i, [128, 256], BF16) for i in range(2)]
    ssqa = P.sb("ssqa", [128, 8, 4], F32)
    psum = [es.enter_context(nc.psum_tensor("ps%d" % i, [128, 512], F32)) for i in range(8)]
    pctr = {"mm": 0, "s": 0, "acc": 0, "s4": 0, "acc4": 0, "mm8": 0}

    def bank(pool):
        base, n = {"mm": (0, 4), "s": (4, 2), "acc": (6, 2), "s4": (0, 4), "acc4": (4, 4), "mm8": (0, 8)}[pool]
        i = pctr[pool]
        pctr[pool] += 1
        return psum[base + i % n]

    ctr = {"w": 0, "p": 0, "sq": 0, "tf": 0, "stg": 0, "sm": 0, "uv": 0, "qn": 0, "hn": 0}

    def rot(lst, k):
        i = ctr[k]
        ctr[k] += 1
        return lst[i % len(lst)]

    def mm(out, lhsT, rhs, start=True, stop=True):
        P.op("pe", lambda e: e.matmul(out, lhsT, rhs, start=start, stop=stop), w=[out], r=[lhsT, rhs])

    def transpose(out, in_, ident):
        P.op("pe", lambda e: e.transpose(out, in_, ident), w=[out], r=[in_, ident])

    def act(out, in_, func, bias=None, scale=None):
        kw = {}
        rr = [in_]
        if bias is not None:
            kw["bias"] = bias
            if not isinstance(bias, float):
                rr.append(bias)
        if scale is not None:
            kw["scale"] = scale
            if not isinstance(scale, float):
                rr.append(scale)
        P.op("act", lambda e: e.activation(out=out, in_=in_, func=func, **kw), w=[out], r=rr)

    def tt(out, a, b, op, eng="dve"):
        P.op(eng, lambda e: e.tensor_tensor(out=out, in0=a, in1=b, op=op), w=[out], r=[a, b])

    def ts(out, a, s1, s2, op0, op1=None, eng="dve"):
        rr = [a] + [s for s in (s1, s2) if s is not None and not isinstance(s, float)]
        if op1 is None:
            P.op(eng, lambda e: e.tensor_scalar(out=out, in0=a, scalar1=s1, scalar2=None, op0=op0), w=[out], r=rr)
        else:
            P.op(eng, lambda e: e.tensor_scalar(out=out, in0=a, scalar1=s1, scalar2=s2, op0=op0, op1=op1), w=[out], r=rr)

    def stt(out, a, s, b, op0, op1, eng="dve"):
        rr = [a, b] + ([] if isinstance(s, float) else [s])
        P.op(eng, lambda e: e.scalar_tensor_tensor(out=out, in0=a, scalar=s, in1=b, op0=op0, op1=op1), w=[out], r=rr)

    def copy(out, in_, eng="dve"):
        if eng == "act":
            act(out, in_, AF.Copy)
        else:
            P.op(eng, lambda e: e.tensor_copy(out=out, in_=in_), w=[out], r=[in_])

    def memset(ap, v, eng="dve"):
        P.op(eng, lambda e: e.memset(ap, v), w=[ap])

    def recip(out, in_):
        P.op("dve", lambda e: e.reciprocal(out=out, in_=in_), w=[out], r=[in_])

    def dma(q, out, in_):
        P.op(q, lambda e: e.dma_start(out=out, in_=in_), w=[out], r=[in_], dma=True)

    def debug(name, ap):
        if name in dbg:
            shp = list(ap.shape)
            dt = ap.dtype
            d = nc.dram_tensor("dbg_" + name, shp, dt, kind="ExternalOutput").ap()
            dma("sp", d, ap)

    def defer_begin():
        P._real_op = P.op
        P._deferred = []
        P.op = lambda *a, **k: P._deferred.append((a, k))

    def defer_end():
        P.op = P._real_op
        return P._deferred

    def drip(lst, n):
        for _ in range(n):
            if not lst:
                return
            a, k = lst.pop(0)
            P.op(*a, **k)

    def chk(name):
        if stop == name:
            raise _Stop()

    def load_w(src, ncols_total):
        K, n = src.shape[1], src.shape[2]
        slot = rot(wring, "w")
        v = slot[:, 0:K * n].rearrange("p (k n) -> p k n", k=K)
        dma("pool", v, src)
        return v

    def cv(l, off, n=1):
        return colv[:, l * LV + off: l * LV + off + n]

    ones_bf = cstb[:, CB_ONES:CB_ONES + 128]
    blk_bf = cstb[:, CB_BLK:CB_BLK + 128]
    ident_bf = cstb[:, CB_ID:CB_ID + 128]
    rmat_bf = cstb[:, CB_R:CB_R + 128]
    maskd = [cstb[:, CB_MF:CB_MF + 128], cstb[:, CB_MB:CB_MB + 128]]
    kaug = cstb[0:16, CB_KA:CB_KA + 1024]
    qaug = cstb[0:16, CB_QA:CB_QA + 1024]
    COS = cstf[:, CF_COS:CF_COS + 1024]
    SIN = cstf[:, CF_SIN:CF_SIN + 1024]
    i8 = cstf[0:8, CF_I8:CF_I8 + 8]
    ones8 = cstf[0:8, CF_ONE8:CF_ONE8 + 128]
    alpha, beta = cstf[0:8, CF_AL:CF_AL + 1], cstf[0:8, CF_BE:CF_BE + 1]
    phi, omphi = cstf[0:8, CF_PHI:CF_PHI + 1], cstf[0:8, CF_OMP:CF_OMP + 1]
    zero8 = cstf[0:8, CF_ZERO:CF_ZERO + 1]
    epsc = cstf[:, CF_EPS:CF_EPS + 1]

    STG0 = 8192

    def sv(off, shape, dt=BF16, rows=None):
        n = _prod(shape[1:])
        if dt == F32:
            a = BIG[:, STG0 + off: STG0 + off + 2 * n] if rows is None else BIG[0:rows, STG0 + off: STG0 + off + 2 * n]
            a = a.bitcast(F32)
        else:
            a = BIG[:, STG0 + off: STG0 + off + n]
        if len(shape) == 3:
            a = a.rearrange("p (a b) -> p a b", a=shape[1])
        elif len(shape) == 4:
            a = a.rearrange("p (a b c) -> p a b c", a=shape[1], b=shape[2])
        return a

    mixT = BIG[:, 0:8192].rearrange("p (k t) -> p k t", k=8)
    hidden = BIG[:, 0:22 * 1024].rearrange("p (k t) -> p k t", k=22)

    dma("sp", cstf[:, :], cstf_d[:, :])
    dma("sp", cstb[:, :], cstb_d[:, :])
    dma("sp", colv[:, :], colv_d[:, :])
    for k in range(8):
        dma("sp", xTs[:, k, :], xs_d[k * 128:(k + 1) * 128, :])
    for k in range(8):
        dma("sp", xTp[:, k, :], xp_d[k * 128:(k + 1) * 128, :])
    act(scbf[:, :, 0], colv[:, 4 * LV:4 * LV + 8], AF.Silu)
    act(scbf[:, :, 1], colv[:, 4 * LV + 8:4 * LV + 16], AF.Silu)
    for s_ in strip:
        memset(s_[:, :, :], 0.0)

    class Path:
        pass

    paths = []
    for j, (xT, T) in enumerate(((xTs, TS), (xTp, TP))):
        p = Path()
        p.j, p.xT, p.T, p.NT, p.NTC = j, xT, T, T // 512, T // 128
        p.sample = (j == 0)
        p.segs = [(0, TS)] if j == 0 else [(0, 256), (256, 512)]
        paths.append(p)

    def tsl(t):
        return slice(t * 512, (t + 1) * 512)

    def rms_bcast(src_fn, n, all_act=False):
        ps = bank("mm")
        for k in range(n):
            s = rot(sqt, "sq")
            if all_act or k % 2 == 0:
                act(s[:, :], src_fn(k), AF.Square)
            else:
                tt(s[:, :], src_fn(k), src_fn(k), ALU.mult)
            mm(ps[:, :], ones_bf, s[:, :], k == 0, k == n - 1)
        act(rstd[:, :], ps[:, :], AF.Ln, bias=epsc)
        act(rstd[:, :], rstd[:, :], AF.Exp, scale=-0.5)

    def norm_mod(p, gs, sh):
        for t in range(p.NT):
            rms_bcast(lambda k: p.xT[:, k, tsl(t)], 8)
            for k in range(8):
                tf = rot(tmpf, "tf")
                tt(tf[:, :], p.xT[:, k, tsl(t)], rstd[:, :], ALU.mult)
                act(hT[:, k, tsl(t)], tf[:, :], AF.Identity, bias=sh[:, k, p.j:p.j + 1], scale=gs[:, k, p.j:p.j + 1])

    def post_norm(p, oT, t, gg):
        rms_bcast(lambda m: oT[:, m, :], 8, all_act=True)
        for m in range(8):
            tf = rot(tmpf, "tf")
            tt(tf[:, :], oT[:, m, :], rstd[:, :], ALU.mult, eng="pool" if m % 2 else "dve")
            stt(p.xT[:, m, tsl(t)], tf[:, :], gg[:, m, p.j:p.j + 1], p.xT[:, m, tsl(t)], ALU.mult, ALU.add)

    nrm_ctr = [0]

    def normalise(acc, out_ap, n=512):
        h0 = 64 * (nrm_ctr[0] % 2)
        nrm_ctr[0] += 1
        recip(rec[h0:h0 + 64, 0:n], acc[64:128, 0:n])
        tt(out_ap, acc[0:64, 0:n], rec[h0:h0 + 64, 0:n], ALU.mult)

    def stage_out(dst, src_ps, n=512, rows=128):
        s = rot(stg, "stg")
        copy(s[0:rows, 0:n], src_ps, eng="act")
        dma("sp", dst, s[0:rows, 0:n])

    def proj_fm(p, w, c0, consume):
        for t in range(p.NT):
            ps = bank("mm8")
            for k in range(8):
                mm(ps[:, :], w[:, k, c0:c0 + 128], hT[:, k, tsl(t)], k == 0, k == 7)
            consume(t, ps)

    class Pipe:
        def __init__(self):
            self.pending = None

        def push(self, qk, rest):
            ctx = qk()
            if self.pending is not None:
                self.pending()
            self.pending = lambda: rest(ctx)

        def flush(self):
            if self.pending is not None:
                self.pending()
            self.pending = None

    def prompt_attn(pipe, p, kT_fn, qT_fn, v_fn, out_fn):
        accb = [None]
        for s in range(2):
            for kc in range(2):
                def qk(s=s, kc=kc):
                    pss = (bank("s4"), bank("s4"))
                    for i in range(2):
                        mm(pss[i][:, 0:256], kT_fn(i, slice(s * 256 + kc * 128, s * 256 + kc * 128 + 128)),
                           qT_fn(i, slice(s * 256, s * 256 + 256)))
                    return pss

                def rest(pss, s=s, kc=kc):
                    if accb[0] is None:
                        accb[0] = (bank("acc4"), bank("acc4"))
                    for i in range(2):
                        acc = accb[0][i]
                        pt = rot(ptl, "p")
                        act(pt[:, 0:256], pss[i][:, 0:256], AF.Exp, scale=0.125)
                        mm(acc[:, s * 256:(s + 1) * 256], v_fn(i, s * 2 + kc), pt[:, 0:256], kc == 0, kc == 1)
                    if s == 1 and kc == 1:
                        for i in range(2):
                            normalise(accb[0][i], out_fn(i, slice(0, 512)))
                pipe.push(qk, rest)

    def mods_load(l, sl):
        wada = w_ada_d[l].rearrange("(k p) n -> p k n", p=128)
        return load_w(wada[:, :, sl * 512:(sl + 1) * 512], 512), ctr["w"]

    def mods_compute(l, sl, ws):
        mods = mods2[l % 2]
        ps = bank("mm")
        for mq in range(4):
            for k in range(8):
                mm(ps[:, 2 * mq:2 * mq + 2], ws[:, k, mq * 128:(mq + 1) * 128], scbf[:, k, :], k == 0, k == 7)
        m0 = sl * 4
        tt(mods[:, m0:m0 + 4, :], ps[:, 0:8].rearrange("p (m j) -> p m j", j=2),
           cv(l, 32 + m0, 4).unsqueeze(2).broadcast_to([128, 4, 2]), ALU.add)

    def mods_slot(l, sl):
        ws, _ = mods_load(l, sl)
        mods_compute(l, sl, ws)

    mstate = {"l": None, "next": 12, "pend": None}

    def mnext(n=1):
        for _ in range(n):
            if mstate["l"] is None:
                return
            if mstate["pend"] is not None:
                sl, ws, at = mstate["pend"]
                assert ctr["w"] - at < len(wring), "adaLN slot evicted from the weight ring before use"
                mods_compute(mstate["l"], sl, ws)
                mstate["pend"] = None
            if mstate["next"] < 12:
                ws, at = mods_load(mstate["l"], mstate["next"])
                mstate["pend"] = (mstate["next"], ws, at)
                mstate["next"] += 1
            elif mstate["pend"] is None:
                return

    def mods_finish(l):
        mods, dsc, wg, nbf = mods2[l % 2], dsc2[l % 2], wg2[l % 2], nbf2[l % 2]
        win = w_in_d[l].rearrange("(k p) n -> p k n", p=128)
        gn = lambda i: cv(l, i * 8, 8).unsqueeze(2).broadcast_to([128, 8, 2])
        stt(dsc[:, 0, :, :], mods[:, 8:16, :], 1.0, gn(0), ALU.add, ALU.mult)
        tt(dsc[:, 1, :, :], mods[:, 16:24, :], gn(1), ALU.mult)
        stt(dsc[:, 2, :, :], mods[:, 32:40, :], 1.0, gn(2), ALU.add, ALU.mult)
        tt(dsc[:, 3, :, :], mods[:, 40:48, :], gn(3), ALU.mult)
        for dst0, src0 in ((0, 2560), (4, 2568), (8, 2564), (12, 2572)):
            dma("pool", wg[:, :, dst0:dst0 + 4], win[:, :, src0:src0 + 4])
        ts(nbf[:, :], cv(l, 85, 1)[0:8, :], -1.0, None, ALU.mult)

    def layers():
      for l in range(n_layers):
        win = w_in_d[l].rearrange("(k p) n -> p k n", p=128)
        wout = w_out_d[l].rearrange("(k p) n -> p k n", p=128)
        wgu = w_gu_d[l].rearrange("(k p) n -> p k n", p=128)
        wdown = w_down_d[l].rearrange("(k p) n -> p k n", p=128)
        wada = w_ada_d[l].rearrange("(k p) n -> p k n", p=128)

        mods, dsc, wg, nbf = mods2[l % 2], dsc2[l % 2], wg2[l % 2], nbf2[l % 2]
        if l == 0:
            for sl in range(12):
                mods_slot(0, sl)
            mods_finish(0)
        gs1, gg1, gs2, gg2 = dsc[:, 0, :, :], dsc[:, 1, :, :], dsc[:, 2, :, :], dsc[:, 3, :, :]
        sh1, sh2 = mods[:, 0:8, :], mods[:, 24:32, :]
        chk('mods')
        for p in paths:
            T, NT, NTC = p.T, p.NT, p.NTC
            if p.sample and l + 1 < n_layers:
                mstate["l"], mstate["next"], mstate["pend"] = l + 1, 0, None
            else:
                mstate["l"] = None
            sfx = '_s' if p.sample else '_p'
            norm_mod(p, gs1, sh1)
            chk('norm1' + sfx)
            if l == 0 and p.sample:
                debug("hT", hT[:, :, :])

            naq = sv(0, [128, 2, 1024])
            nak = sv(2048, [128, 2, 1280])
            nav = sv(4608, [128, 10, 4, 128])
            memset(nav[:, :, :, 64:128], 1.0)
            wsA = load_w(win[:, :, 0:512], 512)
            wsB = load_w(win[:, :, 512:768], 256)

            def build_strips(pr):
                for i in range(2):
                    h = 2 * pr + i
                    st = strip[i]
                    dma("sp", nabst[0:64, :, :], nabT_d[l, h])
                    dma("sp", nabst[64:128, :, :], nabT_d[l, h])
                    act(st[0:64, 3:18, :], nabst[0:64, :, :], AF.Exp)
                    act(st[64:128, 4:19, :], nabst[64:128, :, :], AF.Exp)
            if p.sample:
                build_strips(0)
            for c in range(4):
                dstt = naq if c < 2 else nak

                def cons(t, ps, c=c, dstt=dstt):
                    copy(dstt[:, c % 2, tsl(t)], ps[:, :])
                    if (not p.sample) and c >= 2:
                        stage_out(o_nak[l, c - 2, :, tsl(t)], ps[:, :])
                proj_fm(p, wsA, c * 128, cons)
            for tc in range(NTC):
                ps = bank("mm8")
                for k in range(8):
                    mm(ps[:, 0:256], hT[:, k, tc * 128:(tc + 1) * 128], wsB[:, k, 0:256], k == 0, k == 7)
                copy(nav[:, tc, :, 0:64], ps[:, 0:256].rearrange("p (h d) -> p h d", h=4))
                if not p.sample:
                    stage_out(o_nav[l, tc * 128:(tc + 1) * 128, :], ps[:, 0:256], n=256)
            chk('naproj' + sfx)
            if p.sample:
                for c in range(2):
                    dma("pool", nak[:, c, 1024:1280], na_kT_d[l, c])
                for kc in range(2):
                    dma("pool", nav[:, 8 + kc, :, 0:64],
                        na_v_d[l][:, kc * 128:(kc + 1) * 128, :].rearrange("h s d -> s h d"))
                pipe = Pipe()
                for pr in range(2):
                    c = pr
                    pipe.flush()
                    if pr > 0:
                        build_strips(pr)
                    for qt in range(2):
                        accb = [None]
                        kl = [(j, True) for j in (range(0, 6) if qt == 0 else range(2, 8))] + [(8, False), (9, False)]
                        for it, (kc, w_) in enumerate(kl):
                            def qk(kc=kc, w_=w_, qt=qt, c=c):
                                pss = (bank("s4"), bank("s4"))
                                for i in range(2):
                                    hf = i * 64
                                    mm(pss[i][:, :], nak[hf:hf + 64, c, kc * 128:(kc + 1) * 128], naq[hf:hf + 64, c, tsl(qt)], True, not w_)
                                if w_:
                                    for i in range(2):
                                        mm(pss[i][:, :], kaug[:, kc * 128:(kc + 1) * 128], qaug[:, tsl(qt)], False, True)
                                return pss

                            def rest(pss, kc=kc, w_=w_, qt=qt, c=c, pr=pr, it=it, n=len(kl), accb=accb):
                                if accb[0] is None:
                                    accb[0] = (bank("acc4"), bank("acc4"))
                                for i in range(2):
                                    acc = accb[0][i]
                                    pt = rot(ptl, "p")
                                    act(pt[:, :], pss[i][:, :], AF.Exp, scale=0.125)
                                    if w_:
                                        b0 = 10 - 2 * kc + 8 * qt
                                        tt(pt[:, :].rearrange("p (a b) -> p a b", a=8),
                                           pt[:, :].rearrange("p (a b) -> p a b", a=8), strip[i][:, b0:b0 + 8, :], ALU.mult)
                                    mm(acc[:, :], nav[:, kc, 2 * pr + i, :], pt[:, :], it == 0, it == n - 1)
                                if it == n - 1:
                                    for i in range(2):
                                        normalise(accb[0][i], mixT[i * 64:i * 64 + 64, c, tsl(qt)])
                            pipe.push(qk, rest)
                pipe.flush()
            else:
                pipe = Pipe()
                for pr in range(2):
                    c = pr
                    prompt_attn(pipe, p, lambda i, sl_, c=c: nak[i * 64:i * 64 + 64, c, sl_],
                                lambda i, sl_, c=c: naq[i * 64:i * 64 + 64, c, sl_],
                                lambda i, kc, pr=pr: nav[:, kc, 2 * pr + i, :],
                                lambda i, sl_, c=c: mixT[i * 64:i * 64 + 64, c, sl_])
                pipe.flush()
            if l == 0 and p.sample:
                debug("mix_na", mixT[:, 0:2, :])

            chk('na' + sfx)
            gq = sv(0, [128, 4, 1024])
            gk = sv(4096, [128, 2, 1280])
            gv = sv(6656, [128, 10, 2, 128])
            memset(gv[:, :, :, 64:128], 1.0)
            wsC = load_w(win[:, :, 768:1280], 512)
            slotD = rot(wring, "w")
            wsD = slotD[:, 0:8 * 384].rearrange("p (k n) -> p k n", k=8)
            for d0, s0, n_ in ((0, 1280, 64), (64, 1280, 64), (128, 1344, 64), (192, 1344, 64), (256, 1408, 128)):
                dma("pool", wsD[:, :, d0:d0 + n_], win[:, :, s0:s0 + n_])
            units = [(c, t) for c in range(6) for t in range(NT)]
            st_ = {}

            def g_a(u):
                c, t = u
                w_ = wsC if c < 4 else wsD
                c0 = c * 128 if c < 4 else (c - 4) * 128
                ps = bank("mm")
                for k in range(8):
                    mm(ps[:, :], w_[:, k, c0:c0 + 128], hT[:, k, tsl(t)], k == 0, k == 7)
                st_[u] = {"ps": ps}

            def g_b(u):
                s_q = rot(sqt, "sq")
                act(s_q[:, :], st_[u]["ps"][:, :], AF.Square)
                st_[u]["sq"] = s_q

            def g_c(u):
                ps2 = bank("acc")
                mm(ps2[:, :], blk_bf, st_[u]["sq"][:, :])
                st_[u]["ps2"] = ps2

            def g_d(u):
                act(rstd[:, :], st_[u]["ps2"][:, :], AF.Ln, bias=epsc)
                act(rstd[:, :], rstd[:, :], AF.Exp, scale=-0.5)

            def g_e(u):
                c, t = u
                ps = st_[u]["ps"]
                gcol = cv(l, 80 if c < 4 else 81, 1)
                dstt = gq[:, c, :] if c < 4 else gk[:, c - 4, :]
                if p.sample:
                    q_ = rot(qn2, "qn")
                    stt(q_[:, :], ps[:, :], gcol, rstd[:, :], ALU.mult, ALU.mult)
                    st_[u]["q"] = q_
                else:
                    stt(dstt[:, tsl(t)], ps[:, :], gcol, rstd[:, :], ALU.mult, ALU.mult)
                    if c >= 4:
                        s2 = rot(stg, "stg")
                        stt(s2[:, :], ps[:, :], gcol, rstd[:, :], ALU.mult, ALU.mult)
                        dma("sp", o_gk[l, (c - 4) * 64:(c - 4) * 64 + 64, tsl(t)], s2[0:64, :])

            def g_f(u):
                if not p.sample:
                    return
                ps3 = bank("acc")
                mm(ps3[:, :], rmat_bf, st_[u]["q"][:, :])
                st_[u]["ps3"] = ps3

            def g_g(u):
                c, t = u
                if not p.sample:
                    return
                dstt = gq[:, c, :] if c < 4 else gk[:, c - 4, :]
                q_, ps3 = st_[u]["q"], st_[u]["ps3"]
                tt(tmpf[0][:, :], q_[:, :], COS[:, tsl(t)], ALU.mult)
                tt(tmpf[1][:, :], ps3[:, :], SIN[:, tsl(t)], ALU.mult)
                tt(dstt[:, tsl(t)], tmpf[0][:, :], tmpf[1][:, :], ALU.add)

            nu = len(units)
            U = lambda i: units[i] if 0 <= i < nu else None
            for i in range(nu + 2):
                if U(i):
                    g_a(U(i))
                if U(i - 1):
                    g_c(U(i - 1))
                if U(i - 2):
                    g_f(U(i - 2))
                if U(i):
                    g_b(U(i))
                if U(i - 1):
                    g_d(U(i - 1))
                    g_e(U(i - 1))
                if U(i - 2):
                    g_g(U(i - 2))
            mnext(2)
            for tc in range(NTC):
                ps = bank("mm8")
                for k in range(8):
                    mm(ps[:, 0:128], hT[:, k, tc * 128:(tc + 1) * 128], wsD[:, k, 256:384], k == 0, k == 7)
                copy(gv[:, tc, :, 0:64], ps[:, 0:128].rearrange("p (h d) -> p h d", h=2))
                if not p.sample:
                    stage_out(o_gv[l, tc * 128:(tc + 1) * 128, :], ps[:, 0:128], n=128)
            if p.sample:
                for kv in range(2):
                    dma("pool", gk[:, kv, 1024:1280], gq_kT_d[l, kv])
                for kc in range(2):
                    dma("pool", gv[:, 8 + kc, :, 0:64],
                        gq_v_d[l][:, kc * 128:(kc + 1) * 128, :].rearrange("h s d -> s h d"))
                pipe = Pipe()
                for pr in range(4):
                    kv, c = pr // 2, pr
                    for qt in range(2):
                        accb = [None]
                        for kc in range(10):
                            def qk(kc=kc, qt=qt, kv=kv, c=c):
                                pss = (bank("s4"), bank("s4"))
                                for i in range(2):
                                    hf = i * 64
                                    mm(pss[i][:, :], gk[hf:hf + 64, kv, kc * 128:(kc + 1) * 128], gq[hf:hf + 64, c, tsl(qt)])
                                return pss

                            def rest(pss, kc=kc, qt=qt, kv=kv, c=c, accb=accb):
                                if accb[0] is None:
                                    accb[0] = (bank("acc4"), bank("acc4"))
                                for i in range(2):
                                    acc = accb[0][i]
                                    pt = rot(ptl, "p")
                                    act(pt[:, :], pss[i][:, :], AF.Exp, scale=0.125)
                                    mm(acc[:, :], gv[:, kc, kv, :], pt[:, :], kc == 0, kc == 9)
                                if kc == 9:
                                    for i in range(2):
                                        normalise(accb[0][i], mixT[i * 64:i * 64 + 64, 2 + c, tsl(qt)])
                            pipe.push(qk, rest)
                pipe.flush()
            else:
                pipe = Pipe()
                for pr in range(4):
                    kv, c = pr // 2, pr
                    prompt_attn(pipe, p, lambda i, sl_, kv=kv: gk[i * 64:i * 64 + 64, kv, sl_],
                                lambda i, sl_, c=c: gq[i * 64:i * 64 + 64, c, sl_],
                                lambda i, kc, kv=kv: gv[:, kc, kv, :],
                                lambda i, sl_, c=c: mixT[i * 64:i * 64 + 64, 2 + c, sl_])
                pipe.flush()
            if l == 0 and p.sample:
                debug("mix_gqa", mixT[:, 2:6, :])

            chk('gqa' + sfx)
            mlq = sv(0, [128, 2, 1024])
            mlk = sv(2048, [128, 2, 1024])
            mlkt = sv(4096, [128, 8, 256])
            mlv = sv(6144, [128, 8, 4, 66])
            Htok = sv(8256, [128, 8, 256], F32)
            g1 = sv(8256, [8, 1024], F32, rows=8)
            g2 = sv(8256 + 2048, [8, 1024], F32, rows=8)
            g3 = sv(12352, [8, 1024], F32, rows=8)
            sg = outT0[:, :, :].bitcast(BF16).rearrange("p a b -> p (a b)")[:, 0:2048].rearrange("p (a b) -> p a b", a=2)
            memset(mlv[:, :, :, 64:66], 1.0)
            wsE = load_w(win[:, :, 1536:2048], 512)
            wsF = load_w(win[:, :, 2048:2560], 512)
            for t in range(NT):
                psI = bank("mm")
                for k in range(8):
                    mm(psI[0:8, :], wg[:, k, 0:8], hT[:, k, tsl(t)], k == 0, k == 7)
                psF = bank("mm")
                for k in range(8):
                    mm(psF[0:8, :], wg[:, k, 8:16], hT[:, k, tsl(t)], k == 0, k == 7)
                act(g1[:, tsl(t)], psI[0:8, :], AF.Identity, bias=cv(l, 84, 1)[0:8, :])
                act(g2[:, tsl(t)], psF[0:8, :], AF.Exp, bias=nbf[:, :], scale=-1.0)
            act(g2[:, 0:T], g2[:, 0:T], AF.Ln, bias=1.0)
            defer_begin()
            m0c = cv(l, 86, 1)[0:8, :] if p.sample else zero8
            GS = lambda w_, a, b: gsm[:, w_, a:b]
            CM, GOF, GOB, GIF, GIB, GO, GI, CA, CNF, CNB, CN, TMP, MF = range(13)
            for si, (s0, s1) in enumerate(p.segs):
                sc_o, sc_i = g3[:, s0:s1], g2[:, s0:s1]
                P.op("dve", lambda e, sc_o=sc_o, sc_i=sc_i: e.tensor_tensor_scan(
                    out=sc_o, data0=sc_i, data1=sc_i, initial=0.0, op0=ALU.add, op1=ALU.max),
                    w=[sc_o], r=[sc_i])
                copy(tots[:, si:si + 1], g3[:, s1 - 1:s1])
                ts(g2[:, s0:s1], g2[:, s0:s1], tots[:, si:si + 1], beta, ALU.add, ALU.mult)
                stt(g3[:, s0:s1], g3[:, s0:s1], alpha, g2[:, s0:s1], ALU.mult, ALU.add)
            tt(g1[:, 0:T], g1[:, 0:T], g3[:, 0:T], ALU.subtract)
            cm_o, cm_i = GS(CM, 0, NTC), g1[:, 0:T].rearrange("p (c s) -> p c s", s=128)
            P.op("dve", lambda e, cm_o=cm_o, cm_i=cm_i: e.tensor_reduce(out=cm_o, in_=cm_i, axis=AX.X, op=ALU.max),
                 w=[cm_o], r=[cm_i])
            for si, (s0, s1) in enumerate(p.segs):
                a0, a1 = s0 // 128, s1 // 128
                tt(GS(GOF, a0, a0 + 1), GS(CM, a0, a0 + 1), m0c, ALU.max)
                for c in range(a0 + 1, a1):
                    tt(GS(GOF, c, c + 1), GS(CM, c, c + 1), GS(GOF, c - 1, c), ALU.max)
                tt(GS(GOB, a1 - 1, a1), GS(CM, a1 - 1, a1), m0c, ALU.max)
                for c in range(a1 - 2, a0 - 1, -1):
                    tt(GS(GOB, c, c + 1), GS(CM, c, c + 1), GS(GOB, c + 1, c + 2), ALU.max)
                copy(GS(GIF, a0, a0 + 1), m0c)
                copy(GS(GIF, a0 + 1, a1), GS(GOF, a0, a1 - 1))
                copy(GS(GIB, a1 - 1, a1), m0c)
                copy(GS(GIB, a0, a1 - 1), GS(GOB, a0 + 1, a1))
            blend = lambda o, f, b: (ts(GS(TMP, 0, NTC), GS(f, 0, NTC), phi, None, ALU.mult),
                                     stt(GS(o, 0, NTC), GS(b, 0, NTC), omphi, GS(TMP, 0, NTC), ALU.mult, ALU.add))
            blend(GO, GOF, GOB)
            blend(GI, GIF, GIB)
            tt(GS(CA, 0, NTC), GS(GI, 0, NTC), GS(GO, 0, NTC), ALU.subtract)
            act(GS(CA, 0, NTC), GS(CA, 0, NTC), AF.Exp)
            for si, (s0, s1) in enumerate(p.segs):
                a0, a1 = s0 // 128, s1 // 128
                copy(GS(CNF, a0, a1 - 1), GS(CA, a0 + 1, a1))
                memset(GS(CNF, a1 - 1, a1), 1.0)
                copy(GS(CNB, a0 + 1, a1), GS(CA, a0, a1 - 1))
                memset(GS(CNB, a0, a0 + 1), 1.0)
                if not p.sample:
                    tt(GS(MF, si, si + 1), GS(GOF, a1 - 1, a1), tots[:, si:si + 1], ALU.subtract)
                    dma("sp", o_m[l, si:si + 1, :].rearrange("a r -> r a"), GS(MF, si, si + 1))
            blend(CN, CNF, CNB)
            bc = lambda w_: GS(w_, 0, NTC).unsqueeze(2).broadcast_to([8, NTC, 128])
            v3 = lambda g: g[:, 0:T].rearrange("p (c s) -> p c s", s=128)
            tt(v3(g1), v3(g1), bc(GO), ALU.subtract)
            act(g1[:, 0:T], g1[:, 0:T], AF.Exp)
            tt(v3(g3), v3(g3), bc(GO), ALU.add)
            act(g3[:, 0:T], g3[:, 0:T], AF.Exp, scale=-1.0)
            stt(v3(g2), v3(g1), 0.125, bc(CN), ALU.mult, ALU.mult)
            gl = defer_end()
            for c in range(4):
                dstt = mlq if c < 2 else mlk
                proj_fm(p, wsE, c * 128, lambda t, ps, c=c, dstt=dstt: (copy(dstt[:, c % 2, tsl(t)], ps[:, :]), drip(gl, 3)))
            mnext(1)
            for tc in range(NTC):
                ps = bank("mm8")
                for k in range(8):
                    mm(ps[:, 0:256], hT[:, k, tc * 128:(tc + 1) * 128], wsE[:, k, 256:512], k == 0, k == 7)
                copy(mlkt[:, tc, :], ps[:, 0:256])
                drip(gl, 3)
                ps = bank("mm8")
                for k in range(8):
                    mm(ps[:, 0:256], hT[:, k, tc * 128:(tc + 1) * 128], wsF[:, k, 0:256], k == 0, k == 7)
                copy(mlv[:, tc, :, 0:64], ps[:, 0:256].rearrange("p (h d) -> p h d", h=4))
                drip(gl, 3)
            for c in range(2):
                proj_fm(p, wsF, 256 + c * 128, lambda t, ps, c=c: (act(sg[:, c, tsl(t)], ps[:, :], AF.Sigmoid), drip(gl, 3)))
            drip(gl, 10 ** 6)
            mnext(1)
            pst = bank("mm")
            for xi, g in enumerate((g1, g2, g3)):
                for tc in range(NTC):
                    col = (xi * 8 + tc) * 8
                    mm(pst[:, col:col + 8], g[:, tc * 128:(tc + 1) * 128], i8)
            copy(tokm[:, :, :, :], pst[:, 0:192].rearrange("p (x c r) -> p x c r", x=3, c=8))
            utok, wtok, fltok = tokm[:, 0, :, :], tokm[:, 1, :, :], tokm[:, 2, :, :]
            tt(rhsexp[:, :, 0:NTC], GS(CA, 0, NTC).unsqueeze(1).broadcast_to([8, 8, NTC]),
               i8.unsqueeze(2).broadcast_to([8, 8, NTC]), ALU.mult)
            psc = bank("mm")
            mm(psc[:, 0:64], ones8, rhsexp[:, :, :].rearrange("p a b -> p (a b)"))
            copy(carry_bc[:, :, :], psc[:, 0:64].rearrange("p (a b) -> p a b", a=8))
            if l == 0 and p.sample:
                debug("tokm", tokm[:, :, :, :])
                debug("carry_bc", carry_bc[:, :, :])

            chk('mlgate' + sfx)
            hwritten = set()
            for si, (s0, s1) in enumerate(p.segs):
                a0, a1 = s0 // 128, s1 // 128
                ncs = a1 - a0
                if p.sample:
                    mlc = ml_C_d[l].rearrange("(a two) k j -> two k a j", two=2)
                    c32v = C32[:, :, :].rearrange("p (a two) j -> p two a j", two=2)
                    for par in range(2):
                        dma("sp", c32v[par * 64:par * 64 + 64, par, :, :], mlc[par])
                    for r_ in range(8):
                        cf = a0 if r_ < 4 else a1 - 1
                        hs = slice((r_ % 2) * 64, (r_ % 2) * 64 + 64)
                        ts(C32[hs, r_, :], C32[hs, r_, :], carry_bc[hs, r_, cf:cf + 1], None, ALU.mult)
                        copy(C16[hs, r_, 0:65], C32[hs, r_, :], eng="act")
                else:
                    memset(C32[:, :, :], 0.0)
                    memset(C16[:, :, :], 0.0)
                chk('mlA' + sfx)
                its = [(step, d_) for step in range(ncs) for d_ in range(2)]
                cx = {}

                def ph_a(j):
                    step, d_ = its[j]
                    ch = a0 + step if d_ == 0 else a1 - 1 - step
                    psSb = [bank("s"), bank("s")]
                    psSv = lambda h: psSb[h % 2][:, (h // 2) * 128:(h // 2 + 1) * 128]
                    for h in range(4):
                        hf = (h % 2) * 64
                        mm(psSv(h), mlk[hf:hf + 64, h // 2, ch * 128:(ch + 1) * 128],
                           mlq[hf:hf + 64, h // 2, ch * 128:(ch + 1) * 128])
                    sms, uvs = [], []
                    for h in range(4):
                        r_ = d_ * 4 + h
                        sm = rot(smt, "sm")
                        stt(sm[:, :], psSv(h), utok[:, ch, r_:r_ + 1], maskd[d_], ALU.mult, ALU.mult)
                        uv = rot(uvt, "uv")
                        act(uv[:, 0:65], mlv[:, ch, h, 0:65], AF.Copy, scale=wtok[:, ch, r_:r_ + 1])
                        sms.append(sm)
                        uvs.append(uv)
                    cx[j] = (sms, uvs)

                def ph_b(j):
                    step, d_ = its[j]
                    ch = a0 + step if d_ == 0 else a1 - 1 - step
                    sms, uvs = cx[j]
                    psO = bank("acc")
                    psD = bank("mm")
                    for h in range(4):
                        hf = (h % 2) * 64
                        r_ = d_ * 4 + h
                        mm(psO[:, h * 66:h * 66 + 65], sms[h][:, :], mlv[:, ch, h, 0:65], True, False)
                        mm(psO[:, h * 66:h * 66 + 65], mlq[hf:hf + 64, h // 2, ch * 128:(ch + 1) * 128],
                           C16[hf:hf + 64, r_, 0:65], False, True)
                        mm(psD[0:64, h * 65:(h + 1) * 65], mlkt[:, ch, h * 64:(h + 1) * 64], uvs[h][:, 0:65])
                    cx[j] = (psO, psD)

                def ph_c(j):
                    step, d_ = its[j]
                    ch = a0 + step if d_ == 0 else a1 - 1 - step
                    nx = ch + 1 if d_ == 0 else ch - 1
                    last = step == ncs - 1
                    psO, psD = cx[j]
                    pso3 = psO[:, 0:264].rearrange("p (h j) -> p h j", h=4)
                    ts(den[:, :], pso3[:, :, 64], -1.0, None, ALU.mult)
                    stt(den[:, :], den[:, :], -1.0, den[:, :], ALU.mult, ALU.max)
                    tt(den[:, :], den[:, :], fltok[:, ch, d_ * 4:d_ * 4 + 4], ALU.max)
                    recip(den[:, :], den[:, :])
                    dH = Htok[:, ch, :].rearrange("p (h d) -> p h d", h=4)
                    dbc = den[:, :].unsqueeze(2).broadcast_to([128, 4, 64])
                    if (si, ch) not in hwritten:
                        hwritten.add((si, ch))
                        tt(dH, pso3[:, :, 0:64], dbc, ALU.mult)
                    else:
                        th = tmpH[:, :].rearrange("p (h d) -> p h d", h=4)
                        tt(th, pso3[:, :, 0:64], dbc, ALU.mult)
                        tt(dH, dH, th, ALU.add)
                    for h in range(4):
                        r_ = d_ * 4 + h
                        hs = slice((h % 2) * 64, (h % 2) * 64 + 64)
                        pd = psD[0:64, h * 65:(h + 1) * 65]
                        if not last:
                            stt(C32[hs, r_, :], C32[hs, r_, :], carry_bc[hs, r_, nx:nx + 1], pd, ALU.mult, ALU.add)
                            copy(C16[hs, r_, 0:65], C32[hs, r_, :], eng="act")
                        else:
                            tt(C32[hs, r_, :], C32[hs, r_, :], pd, ALU.add)

                ph_a(0)
                for j in range(len(its)):
                    if j + 1 < len(its):
                        ph_a(j + 1)
                    ph_b(j)
                    ph_c(j)
                    if j % 2 == 1:
                        mnext(1)
                if not p.sample:
                    ocv = o_C[l, si].rearrange("(a two) k j -> two k a j", two=2)
                    c32v = C32[:, :, :].rearrange("p (a two) j -> p two a j", two=2)
                    for par in range(2):
                        dma("sp", ocv[par], c32v[par * 64:par * 64 + 64, par, :, :])
            if l == 0 and p.sample:
                debug("Htok", Htok[:, :, :])
            chk('mlloop' + sfx)
            for tc in range(NTC):
                Hc = Htok[:, tc, :]
                tt(tmpH[:, :], Hc, Hc, ALU.mult)
                sq_o, sq_i = ssqa[:, tc, :], tmpH[:, :].rearrange("p (h d) -> p h d", h=4)
                P.op("dve", lambda e, sq_o=sq_o, sq_i=sq_i: e.tensor_reduce(out=sq_o, in_=sq_i, axis=AX.X, op=ALU.add),
                     w=[sq_o], r=[sq_i])
            act(ssqa[:, 0:NTC, :], ssqa[:, 0:NTC, :], AF.Ln, bias=epsc, scale=1.0 / 64)
            act(ssqa[:, 0:NTC, :], ssqa[:, 0:NTC, :], AF.Exp, scale=-0.5)
            for tc in range(NTC):
                Hc = Htok[:, tc, :]
                hnb = rot(hnb2, "hn")
                tt(hnb[:, :].rearrange("p (h d) -> p h d", h=4), Hc.rearrange("p (h d) -> p h d", h=4),
                   ssqa[:, tc, :].unsqueeze(2).broadcast_to([128, 4, 64]), ALU.mult)
                for c in range(2):
                    pT = bank("mm")
                    pTb = pT[:, :].bitcast(BF16)
                    transpose(pTb[:, 0:128], hnb[:, c * 128:(c + 1) * 128], ident_bf)
                    stt(mixT[:, 6 + c, tc * 128:(tc + 1) * 128], pTb[:, 0:128], cv(l, 82 + c, 1),
                        sg[:, c, tc * 128:(tc + 1) * 128], ALU.mult, ALU.mult)
            if l == 0 and p.sample:
                debug("mixT", mixT[:, :, :])

            chk('ml' + sfx)
            wo = [load_w(wout[:, :, i * 512:(i + 1) * 512], 512) for i in range(2)]
            for t in range(NT):
                oT = outT0 if t == 0 else outT1
                for m in range(8):
                    ps = bank("mm8")
                    for k in range(8):
                        mm(ps[:, :], wo[m // 4][:, k, (m % 4) * 128:(m % 4 + 1) * 128], mixT[:, k, tsl(t)], k == 0, k == 7)
                    copy(oT[:, m, :], ps[:, :], eng="act" if m % 2 else "dve")
                post_norm(p, oT, t, gg1)
            if l == 0 and p.sample:
                debug("x_mid", xTs[:, :, :])

            chk('wout' + sfx)
            norm_mod(p, gs2, sh2)
            if mstate["l"] is not None:
                mnext(14)
                assert mstate["pend"] is None and mstate["next"] == 12
                mods_finish(mstate["l"])
                mstate["l"] = None
            for s_ in range(11):
                slot = rot(wring, "w")
                wv = slot[:, :].rearrange("p (k n) -> p k n", k=8)
                dma("pool", wv[:, :, 0:256], wgu[:, :, s_ * 256:(s_ + 1) * 256])
                dma("pool", wv[:, :, 256:512], wgu[:, :, FF + s_ * 256:FF + (s_ + 1) * 256])
                for jj in range(2):
                    j = s_ * 2 + jj
                    for t in range(NT):
                        psg = bank("mm")
                        for k in range(8):
                            mm(psg[:, :], wv[:, k, jj * 128:(jj + 1) * 128], hT[:, k, tsl(t)], k == 0, k == 7)
                        psu = bank("mm")
                        for k in range(8):
                            mm(psu[:, :], wv[:, k, 256 + jj * 128:256 + (jj + 1) * 128], hT[:, k, tsl(t)], k == 0, k == 7)
                        pt = rot(ptl, "p")
                        act(pt[:, :], psg[:, :], AF.Silu)
                        tt(hidden[:, j, tsl(t)], psu[:, :], pt[:, :], ALU.mult)
            for m in range(8):
                wd = load_w(wdown[:, :, m * 128:(m + 1) * 128], 128)
                for t in range(NT):
                    oT = outT0 if t == 0 else outT1
                    ps = bank("mm8")
                    for j in range(22):
                        mm(ps[:, :], wd[:, j, :], hidden[:, j, tsl(t)], j == 0, j == 21)
                    copy(oT[:, m, :], ps[:, :], eng="act" if m % 2 else "dve")
            for t in range(NT):
                post_norm(p, outT0 if t == 0 else outT1, t, gg2)
            chk('ffn' + sfx)
            if p.sample and l == n_layers - 1:
                for k in range(8):
                    dma("sp", ys_d[k * 128:(k + 1) * 128, :], xTs[:, k, :])

    try:
        layers()
    except _Stop:
        pass
    for k in range(8):
        if stop is not None:
            dma("sp", ys_d[k * 128:(k + 1) * 128, :], xTs[:, k, :])
        dma("sp", yp_d[k * 128:(k + 1) * 128, :], xTp[:, k, :])
    P.emit()
    es.close()
    return nc


def _consts():
    import ml_dtypes
    cf = np.zeros((128, NF), np.float32)
    t = np.arange(1024)
    row, col = (t // 64).astype(np.float64), (t % 64).astype(np.float64)
    inv = 1.0 / (10000.0 ** (np.arange(16, dtype=np.float64) / 16))
    for p in range(128):
        j = p % 64
        pos = row if j < 32 else col
        hd = j % 32
        ang = pos * inv[hd % 16]
        cf[p, CF_COS:CF_COS + 1024] = np.cos(ang)
        cf[p, CF_SIN:CF_SIN + 1024] = (-np.sin(ang) if hd < 16 else np.sin(ang))
    cf[:, CF_ID:CF_ID + 128] = np.eye(128)
    cf[0:4, CF_AL], cf[4:8, CF_AL] = -1.0, 1.0
    cf[0:4, CF_BE], cf[4:8, CF_BE] = 0.0, -1.0
    cf[0:4, CF_PHI], cf[4:8, CF_OMP] = 1.0, 1.0
    cf[0:8, CF_I8:CF_I8 + 8] = np.eye(8)
    cf[0:8, CF_ONE8:CF_ONE8 + 128] = 1.0
    cf[:, CF_EPS] = EPS
    cb = np.zeros((128, NB), np.float32)
    cb[:, CB_ID:CB_ID + 128] = np.eye(128)
    cb[:, CB_ONES:CB_ONES + 128] = 1.0 / 1024
    cb[0:64, CB_BLK:CB_BLK + 64] = 1.0 / 64
    cb[64:128, CB_BLK + 64:CB_BLK + 128] = 1.0 / 64
    for p in range(128):
        partner = p + 16 if (p % 32) < 16 else p - 16
        cb[partner, CB_R + p] = 1.0
    s = np.arange(128)
    cb[:, CB_MF:CB_MF + 128] = 0.125 * (s[:, None] <= s[None, :])
    cb[:, CB_MB:CB_MB + 128] = 0.125 * (s[:, None] >= s[None, :])
    rk = t // 64
    for r in range(16):
        cb[r, CB_KA:CB_KA + 1024] = (rk == r)
        r0 = np.clip(rk - 4, 0, 8)
        inwin = (r >= r0) & (r < r0 + 8)
        cb[r, CB_QA:CB_QA + 1024] = np.where(inwin, 0.0, NEG)
    return cf, cb.astype(ml_dtypes.bfloat16)


def _na_toeplitz(na_bias):
    cq = np.arange(64)
    c0 = np.clip(cq - 8, 0, 48)
    ck = np.arange(64)
    mask = (ck[None, :] >= c0[:, None]) & (ck[None, :] < c0[:, None] + 16)
    dc = np.clip(ck[None, :] - cq[:, None], -15, 15) + 15
    g = na_bias[:, :, ::-1, :][:, :, :, dc]
    g = np.where(mask[None, None, None], g, np.float32(NEG))
    return np.ascontiguousarray(g.transpose(0, 1, 4, 2, 3)).astype(np.float32)


_CACHE = {}


def kernel(x_prompt, x_sample, cache_na_kv, cache_gqa_kv, state_mlstm_C, state_mlstm_n, state_mlstm_m,
           c, c_ctx, w_in, b_gates, w_out, g_norm, g_qk, g_mlstm, na_bias, w_ada, b_ada, w_gu, w_down,
           _n_layers=DEPTH, _dbg=None, _stop=None, _cores=8):
    f = lambda a: np.ascontiguousarray(np.asarray(a, dtype=np.float32))
    x_prompt, x_sample = f(x_prompt), f(x_sample)
    cache_na_kv, cache_gqa_kv = f(cache_na_kv), f(cache_gqa_kv)
    sC, sn, sm = f(state_mlstm_C), f(state_mlstm_n), f(state_mlstm_m)
    c, c_ctx, b_gates, g_norm, g_qk, g_mlstm, na_bias, b_ada = map(f, (c, c_ctx, b_gates, g_norm, g_qk, g_mlstm, na_bias, b_ada))
    key = (_n_layers, tuple(_dbg or ()), _stop)
    if key not in _CACHE:
        _CACHE[key] = build(_n_layers, _dbg, _stop)
    nc = _CACHE[key]
    cf, cb = _consts()
    nabT = _na_toeplitz(na_bias)
    shared = dict(w_in=f(w_in), w_out=f(w_out), w_gu=f(w_gu), w_down=f(w_down), w_ada=f(w_ada),
                  cstf=cf, cstb=cb, nabT=nabT)
    fm = lambda v: v.reshape(-1, 128).T
    in_maps = []
    for b in range(_cores):
        cvv = np.zeros((128, NV), np.float32)
        for l in range(DEPTH):
            o = l * LV
            cvv[:, o:o + 32] = fm(g_norm[l].reshape(-1))
            cvv[:, o + 32:o + 80] = fm(b_ada[l])
            cvv[:, o + 80] = np.tile(g_qk[l, 0], 2)
            cvv[:, o + 81] = np.tile(g_qk[l, 1], 2)
            cvv[:, o + 82:o + 84] = fm(g_mlstm[l])
            cvv[0:8, o + 84] = np.concatenate([b_gates[l, 0:4], b_gates[l, 8:12]])
            cvv[0:8, o + 85] = np.concatenate([b_gates[l, 4:8], b_gates[l, 12:16]])
            cvv[0:8, o + 86] = sm[b, l].reshape(8)
        cvv[:, 4 * LV:4 * LV + 8] = fm(c[b])
        cvv[:, 4 * LV + 8:4 * LV + 16] = fm(c_ctx)
        na_kT = cache_na_kv[b, :, 0].transpose(0, 1, 3, 2).reshape(DEPTH, 2, 128, 256)
        gkT = cache_gqa_kv[b, :, 0].transpose(0, 1, 3, 2)
        gkT = np.concatenate([gkT, gkT], axis=2)
        mlC = np.concatenate([sC[b].transpose(0, 1, 2, 4, 3), sn[b][..., None]], axis=-1).reshape(DEPTH, 8, 64, 65)
        m = dict(shared)
        m.update(xs=np.ascontiguousarray(x_sample[b].T),
                 xp=np.ascontiguousarray(x_prompt[2 * b:2 * b + 2].reshape(TP, D).T),
                 colv=cvv, na_kT=np.ascontiguousarray(na_kT), na_v=np.ascontiguousarray(cache_na_kv[b, :, 1]),
                 gq_kT=np.ascontiguousarray(gkT), gq_v=np.ascontiguousarray(cache_gqa_kv[b, :, 1]),
                 ml_C=np.ascontiguousarray(mlC))
        in_maps.append(m)
    res = run_bass_kernel_spmd(nc, in_maps, core_ids=list(range(_cores)))
    R = res.results
    y_p = np.zeros((16, 256, D), np.float32)
    y_s = np.zeros((8, TS, D), np.float32)
    nna = np.zeros((16, DEPTH, 2, 4, 256, 64), np.float32)
    ngq = np.zeros((16, DEPTH, 2, 2, 256, 64), np.float32)
    nC = np.zeros((16, DEPTH, 2, 4, 64, 64), np.float32)
    nn = np.zeros((16, DEPTH, 2, 4, 64), np.float32)
    nm = np.zeros((16, DEPTH, 2, 4), np.float32)
    for b in range(_cores):
        r = R[b]
        y_s[b] = r["ys"].T
        y_p[2 * b:2 * b + 2] = r["yp"].T.reshape(2, 256, D)
        nak = r["o_nak"].reshape(DEPTH, 4, 64, 2, 256)
        nav = r["o_nav"].reshape(DEPTH, 2, 256, 4, 64)
        gkk = r["o_gk"].reshape(DEPTH, 2, 64, 2, 256)
        gvv = r["o_gv"].reshape(DEPTH, 2, 256, 2, 64)
        oC = r["o_C"].reshape(DEPTH, 2, 2, 4, 64, 65)
        om = r["o_m"].reshape(DEPTH, 2, 2, 4)
        for s in range(2):
            nna[2 * b + s, :, 0] = nak[:, :, :, s, :].transpose(0, 1, 3, 2)
            nna[2 * b + s, :, 1] = nav[:, s].transpose(0, 2, 1, 3)
            ngq[2 * b + s, :, 0] = gkk[:, :, :, s, :].transpose(0, 1, 3, 2)
            ngq[2 * b + s, :, 1] = gvv[:, s].transpose(0, 2, 1, 3)
            nC[2 * b + s] = oC[:, s, :, :, :, 0:64].transpose(0, 1, 2, 4, 3)
            nn[2 * b + s] = oC[:, s, :, :, :, 64]
            nm[2 * b + s] = om[:, s]
    kernel._last = R
    return (y_p, y_s, nna, ngq, nC, nn, nm)
```

```python
import numpy as np
from contextlib import ExitStack
import concourse.bass as bass
import concourse.mybir as mybir
from concourse.bass_utils import run_bass_kernel_spmd

F32 = mybir.dt.float32
BF16 = mybir.dt.bfloat16
ALU = mybir.AluOpType
AF = mybir.ActivationFunctionType
AX = mybir.AxisListType

D = 1024
DEPTH = 4
TS = 1024
TP = 512
EPS = 1e-6
NEG = -1e30
FF = 2816
IN_W = 2576
LV = 88
NV = 4 * LV + 16
CF_COS, CF_SIN, CF_ID, CF_AL, CF_BE, CF_PHI, CF_OMP, CF_I8, CF_ONE8, CF_ZERO, CF_EPS, NF = 0, 1024, 2048, 2176, 2177, 2178, 2179, 2180, 2188, 2316, 2317, 2318
CB_ID, CB_ONES, CB_BLK, CB_R, CB_MF, CB_MB, CB_KA, CB_QA, NB = 0, 128, 256, 384, 512, 640, 768, 1792, 2816


def _prod(s):
    r = 1
    for x in s:
        r *= int(x)
    return r


STRICT = True


class Prog:
    ENGS = ("pe", "act", "dve", "pool", "sp")
    NSLOT = 8

    def __init__(self, nc, es):
        self.nc, self.es = nc, es
        self.ops = {e: [] for e in self.ENGS}
        self.recs = {}
        self.fbytes = {}
        self.waited = {e: {} for e in self.ENGS}
        self.ndma = {"sp": 0, "pool": 0, "act": 0}

    def sb(self, name, shape, dt):
        t = self.es.enter_context(self.nc.sbuf_tensor(name, list(shape), dt))
        self.fbytes[name] = _prod(shape[1:]) * mybir.dt.size(dt)
        return t

    def region(self, ap):
        sp = str(ap.space)
        name = ap.tensor.name
        ds = mybir.dt.size(ap.dtype)
        pat = ap.ap
        off = int(ap.offset)
        if "PSUM" in sp:
            return (name, 0, 128, 0, 1 << 40)
        if "DRAM" in sp:
            hi = off + sum((c - 1) * abs(s) for s, c in pat) + 1
            return (name, 0, 1, off * ds, hi * ds)
        fb = self.fbytes[name]
        ob = off * ds
        p0, f0 = ob // fb, ob % fb
        pstep, pcnt = pat[0]
        assert pcnt == 1 or pstep * ds == fb, (name, pat, fb)
        ext = sum((c - 1) * abs(s) for s, c in pat[1:]) * ds + ds
        return (name, p0, p0 + pcnt, f0, f0 + ext)

    @staticmethod
    def _ov(a, b):
        return a[1] < b[2] and b[1] < a[2] and a[3] < b[4] and b[3] < a[4]

    @staticmethod
    def _inside(a, b):
        return a[1] >= b[1] and a[2] <= b[2] and a[3] >= b[3] and a[4] <= b[4]

    def op(self, eng, fn, w=(), r=(), dma=False):
        idx = len(self.ops[eng])
        deps = []
        if dma:
            d = self.ndma[eng]
            self.ndma[eng] += 1
            slot, use = d % self.NSLOT, d // self.NSLOT
            tok = ("d", eng, slot, use)
            if use > 0:
                deps.append((("d", eng, slot, use - 1), "raw"))
        else:
            tok = ("e", eng, idx)
        key = tok[:3] if dma else tok[:2]
        for ap in r:
            reg = self.region(ap)
            lst = self.recs.setdefault(reg[0], [])
            psum_ap = "PSUM" in str(ap.space)
            for rec in lst:
                if rec[0] == "w" and self._ov(rec[2], reg):
                    deps.append((rec[1], "raw"))
                elif psum_ap and rec[0] == "r" and rec[3] != key and self._ov(rec[2], reg):
                    deps.append((rec[1], "rar"))
            for rec in lst:
                if rec[0] == "r" and rec[3] == key and rec[2] == reg:
                    rec[1] = tok
                    break
            else:
                lst.append(["r", tok, reg, key])
        for ap in w:
            reg = self.region(ap)
            lst = self.recs.setdefault(reg[0], [])
            keep = []
            for rec in lst:
                if self._ov(rec[2], reg):
                    deps.append((rec[1], "waw" if rec[0] == "w" else "war"))
                    if self._inside(rec[2], reg):
                        continue
                keep.append(rec)
            keep.append(["w", tok, reg, key])
            self.recs[reg[0]] = keep
        waits = []
        wd = self.waited[eng]
        for t, hz in deps:
            if t == tok:
                continue
            if t[0] == "e":
                if t[1] == eng and not dma:
                    if eng == "pe" or (hz != "raw" and not STRICT):
                        continue
                k, v = ("e", t[1]), t[2]
            else:
                k, v = t[:3], t[3]
            if wd.get(k, -1) >= v:
                continue
            wd[k] = v
            waits.append(t)
            if t[0] == "e":
                self.ops[t[1]][t[2]]["inc"] = True
        self.ops[eng].append(dict(fn=fn, waits=waits, inc=False, dma=tok if dma else None))

    def emit(self):
        nc, es = self.nc, self.es
        sems = {e: es.enter_context(nc.semaphore("s_" + e)) for e in ("pe", "act", "dve", "pool")}
        dsem = {q: [es.enter_context(nc.semaphore("d_%s%d" % (q, i))) for i in range(self.NSLOT)]
                for q in ("sp", "pool", "act")}
        fin = []
        for q, n in self.ndma.items():
            for s in range(min(n, self.NSLOT)):
                uses = (n - 1 - s) // self.NSLOT
                fin.append(("d", q, s, uses))
        self.ops["sp"].append(dict(fn=None, waits=fin, inc=False, dma=None))
        for e in self.ENGS:
            c = 0
            for o in self.ops[e]:
                if o["inc"]:
                    c += 1
                o["cnt"] = c

        def val(t):
            if t[0] == "e":
                return sems[t[1]], self.ops[t[1]][t[2]]["cnt"]
            return dsem[t[1]][t[2]], 16 * (t[3] + 1)

        def run(name, e):
            for o in self.ops[name]:
                for t in o["waits"]:
                    s, v = val(t)
                    e.wait_ge(s, v)
                if o["fn"] is None:
                    continue
                ins = o["fn"](e)
                if o["dma"] is not None:
                    ins.then_inc(dsem[o["dma"][1]][o["dma"][2]], 16)
                elif o["inc"]:
                    ins.then_inc(sems[name], 1)

        with nc.Block() as block:
            @block.tensor
            def _(e):
                run("pe", e)

            @block.scalar
            def _(e):
                run("act", e)

            @block.vector
            def _(e):
                run("dve", e)

            @block.gpsimd
            def _(e):
                run("pool", e)

            @block.sync
            def _(e):
                run("sp", e)


class _Stop(Exception):
    pass


def build(n_layers=DEPTH, dbg=None, stop=None):
    nc = bass.Bass("TRN2", target_bir_lowering=False)
    es = ExitStack()
    P = Prog(nc, es)
    dbg = dbg or []

    def din(name, shape):
        return nc.dram_tensor(name, list(shape), F32, kind="ExternalInput").ap()

    def dout(name, shape):
        return nc.dram_tensor(name, list(shape), F32, kind="ExternalOutput").ap()

    xs_d, xp_d = din("xs", [D, TS]), din("xp", [D, TP])
    w_in_d, w_out_d = din("w_in", [DEPTH, D, IN_W]), din("w_out", [DEPTH, D, D])
    w_gu_d, w_down_d = din("w_gu", [DEPTH, D, 2 * FF]), din("w_down", [DEPTH, FF, D])
    w_ada_d = din("w_ada", [DEPTH, D, 6 * D])
    colv_d, cstf_d = din("colv", [128, NV]), din("cstf", [128, NF])
    cstb_d = nc.dram_tensor("cstb", [128, NB], BF16, kind="ExternalInput").ap()
    nabT_d = din("nabT", [DEPTH, 4, 64, 15, 64])
    na_kT_d, na_v_d = din("na_kT", [DEPTH, 2, 128, 256]), din("na_v", [DEPTH, 4, 256, 64])
    gq_kT_d, gq_v_d = din("gq_kT", [DEPTH, 2, 128, 256]), din("gq_v", [DEPTH, 2, 256, 64])
    ml_C_d = din("ml_C", [DEPTH, 8, 64, 65])
    ys_d, yp_d = dout("ys", [D, TS]), dout("yp", [D, TP])
    o_nak, o_nav = dout("o_nak", [DEPTH, 2, 128, TP]), dout("o_nav", [DEPTH, TP, 256])
    o_gk, o_gv = dout("o_gk", [DEPTH, 128, TP]), dout("o_gv", [DEPTH, TP, 128])
    o_C, o_m = dout("o_C", [DEPTH, 2, 8, 64, 65]), dout("o_m", [DEPTH, 2, 8])

    xTs = P.sb("xTs", [128, 8, TS], F32)
    xTp = P.sb("xTp", [128, 8, TP], F32)
    hT = P.sb("hT", [128, 8, TS], BF16)
    BIGN = 23552
    BIG = P.sb("BIG", [128, BIGN], BF16)
    outT0 = P.sb("outT0", [128, 8, 512], F32)
    outT1 = hT[:, :, :].bitcast(F32)
    wring = [P.sb("wr%d" % i, [128, 4096], BF16) for i in range(3)]
    cstf = P.sb("cstf_s", [128, NF], F32)
    cstb = P.sb("cstb_s", [128, NB], BF16)
    colv = P.sb("colv_s", [128, NV], F32)
    g1024 = P.sb("g1024", [128, DEPTH, 4, 8], F32)
    sqt = [P.sb("sq%d" % i, [128, 512], BF16) for i in range(2)]
    rstd = P.sb("rstd", [128, 512], F32)
    tmpf = [P.sb("tmpf%d" % i, [128, 512], F32) for i in range(2)]
    ptl = [P.sb("pt%d" % i, [128, 512], BF16) for i in range(4)]
    rec = P.sb("rec", [128, 512], F32)
    stg = [P.sb("stg%d" % i, [128, 512], F32) for i in range(2)]
    mods2 = [P.sb("mods%d" % i, [128, 48, 2], F32) for i in range(2)]
    dsc2 = [P.sb("dsc%d" % i, [128, 4, 8, 2], F32) for i in range(2)]
    scbf = P.sb("scbf", [128, 8, 2], BF16)
    strip = [P.sb("strip%d" % i, [128, 22, 64], BF16) for i in range(2)]
    nabst = P.sb("nabst", [128, 15, 64], F32)
    wg2 = [P.sb("wg%d" % i, [128, 8, 16], BF16) for i in range(2)]
    qn2 = [P.sb("qn%d" % i, [128, 512], BF16) for i in range(2)]
    gsm = P.sb("gsm", [8, 16, 8], F32)
    tots = P.sb("tots", [8, 2], F32)
    nbf2 = [P.sb("nbf%d" % i, [8, 1], F32) for i in range(2)]
    rhsexp = P.sb("rhsexp", [8, 8, 8], F32)
    tokm = P.sb("tokm", [128, 3, 8, 8], F32)
    carry_bc = P.sb("carry_bc", [128, 8, 8], F32)
    C32 = P.sb("C32", [128, 8, 65], F32)
    C16 = P.sb("C16", [128, 8, 66], BF16)
    smt = [P.sb("smt%d" % i, [128, 128], BF16) for i in range(8)]
    uvt = [P.sb("uvt%d" % i, [128, 66], BF16) for i in range(8)]
    den = P.sb("den", [128, 4], F32)
    tmpH = P.sb("tmpH", [128, 256], F32)
    hnb2 = [P.sb("hnb%d" % i, [128, 256], BF16) for i in range(2)]
    ssqa = P.sb("ssqa", [128, 8, 4], F32)
    psum = [es.enter_context(nc.psum_tensor("ps%d" % i, [128, 512], F32)) for i in range(8)]
    pctr = {"mm": 0, "s": 0, "acc": 0, "s4": 0, "acc4": 0, "mm8": 0}

    def bank(pool):
        base, n = {"mm": (0, 4), "s": (4, 2), "acc": (6, 2), "s4": (0, 4), "acc4": (4, 4), "mm8": (0, 8)}[pool]
        i = pctr[pool]
        pctr[pool] += 1
        return psum[base + i % n]

    ctr = {"w": 0, "p": 0, "sq": 0, "tf": 0, "stg": 0, "sm": 0, "uv": 0, "qn": 0, "hn": 0}

    def rot(lst, k):
        i = ctr[k]
        ctr[k] += 1
        return lst[i % len(lst)]

    def mm(out, lhsT, rhs, start=True, stop=True):
        P.op("pe", lambda e: e.matmul(out, lhsT, rhs, start=start, stop=stop), w=[out], r=[lhsT, rhs])

    def transpose(out, in_, ident):
        P.op("pe", lambda e: e.transpose(out, in_, ident), w=[out], r=[in_, ident])

    def act(out, in_, func, bias=None, scale=None):
        kw = {}
        rr = [in_]
        if bias is not None:
            kw["bias"] = bias
            if not isinstance(bias, float):
                rr.append(bias)
        if scale is not None:
            kw["scale"] = scale
            if not isinstance(scale, float):
                rr.append(scale)
        P.op("act", lambda e: e.activation(out=out, in_=in_, func=func, **kw), w=[out], r=rr)

    def tt(out, a, b, op, eng="dve"):
        P.op(eng, lambda e: e.tensor_tensor(out=out, in0=a, in1=b, op=op), w=[out], r=[a, b])

    def ts(out, a, s1, s2, op0, op1=None, eng="dve"):
        rr = [a] + [s for s in (s1, s2) if s is not None and not isinstance(s, float)]
        if op1 is None:
            P.op(eng, lambda e: e.tensor_scalar(out=out, in0=a, scalar1=s1, scalar2=None, op0=op0), w=[out], r=rr)
        else:
            P.op(eng, lambda e: e.tensor_scalar(out=out, in0=a, scalar1=s1, scalar2=s2, op0=op0, op1=op1), w=[out], r=rr)

    def stt(out, a, s, b, op0, op1, eng="dve"):
        rr = [a, b] + ([] if isinstance(s, float) else [s])
        P.op(eng, lambda e: e.scalar_tensor_tensor(out=out, in0=a, scalar=s, in1=b, op0=op0, op1=op1), w=[out], r=rr)

    def copy(out, in_, eng="dve"):
        if eng == "act":
            act(out, in_, AF.Copy)
        else:
            P.op(eng, lambda e: e.tensor_copy(out=out, in_=in_), w=[out], r=[in_])

    def memset(ap, v, eng="dve"):
        P.op(eng, lambda e: e.memset(ap, v), w=[ap])

    def recip(out, in_):
        P.op("dve", lambda e: e.reciprocal(out=out, in_=in_), w=[out], r=[in_])

    def dma(q, out, in_):
        P.op(q, lambda e: e.dma_start(out=out, in_=in_), w=[out], r=[in_], dma=True)

    def debug(name, ap):
        if name in dbg:
            shp = list(ap.shape)
            dt = ap.dtype
            d = nc.dram_tensor("dbg_" + name, shp, dt, kind="ExternalOutput").ap()
            dma("sp", d, ap)

    def defer_begin():
        P._real_op = P.op
        P._deferred = []
        P.op = lambda *a, **k: P._deferred.append((a, k))

    def defer_end():
        P.op = P._real_op
        return P._deferred

    def drip(lst, n):
        for _ in range(n):
            if not lst:
                return
            a, k = lst.pop(0)
            P.op(*a, **k)

    def chk(name):
        if stop == name:
            raise _Stop()

    def load_w(src, ncols_total):
        K, n = src.shape[1], src.shape[2]
        slot = rot(wring, "w")
        v = slot[:, 0:K * n].rearrange("p (k n) -> p k n", k=K)
        dma("pool", v, src)
        return v

    def cv(l, off, n=1):
        return colv[:, l * LV + off: l * LV + off + n]

    ones_bf = cstb[:, CB_ONES:CB_ONES + 128]
    blk_bf = cstb[:, CB_BLK:CB_BLK + 128]
    ident_bf = cstb[:, CB_ID:CB_ID + 128]
    rmat_bf = cstb[:, CB_R:CB_R + 128]
    maskd = [cstb[:, CB_MF:CB_MF + 128], cstb[:, CB_MB:CB_MB + 128]]
    kaug = cstb[0:16, CB_KA:CB_KA + 1024]
    qaug = cstb[0:16, CB_QA:CB_QA + 1024]
    COS = cstf[:, CF_COS:CF_COS + 1024]
    SIN = cstf[:, CF_SIN:CF_SIN + 1024]
    i8 = cstf[0:8, CF_I8:CF_I8 + 8]
    ones8 = cstf[0:8, CF_ONE8:CF_ONE8 + 128]
    alpha, beta = cstf[0:8, CF_AL:CF_AL + 1], cstf[0:8, CF_BE:CF_BE + 1]
    phi, omphi = cstf[0:8, CF_PHI:CF_PHI + 1], cstf[0:8, CF_OMP:CF_OMP + 1]
    zero8 = cstf[0:8, CF_ZERO:CF_ZERO + 1]
    epsc = cstf[:, CF_EPS:CF_EPS + 1]

    STG0 = 8192

    def sv(off, shape, dt=BF16, rows=None):
        n = _prod(shape[1:])
        if dt == F32:
            a = BIG[:, STG0 + off: STG0 + off + 2 * n] if rows is None else BIG[0:rows, STG0 + off: STG0 + off + 2 * n]
            a = a.bitcast(F32)
        else:
            a = BIG[:, STG0 + off: STG0 + off + n]
        if len(shape) == 3:
            a = a.rearrange("p (a b) -> p a b", a=shape[1])
        elif len(shape) == 4:
            a = a.rearrange("p (a b c) -> p a b c", a=shape[1], b=shape[2])
        return a

    mixT = BIG[:, 0:8192].rearrange("p (k t) -> p k t", k=8)
    hidden = BIG[:, 0:22 * 1024].rearrange("p (k t) -> p k t", k=22)

    dma("sp", cstf[:, :], cstf_d[:, :])
    dma("sp", cstb[:, :], cstb_d[:, :])
    dma("sp", colv[:, :], colv_d[:, :])
    for k in range(8):
        dma("sp", xTs[:, k, :], xs_d[k * 128:(k + 1) * 128, :])
    for k in range(8):
        dma("sp", xTp[:, k, :], xp_d[k * 128:(k + 1) * 128, :])
    act(scbf[:, :, 0], colv[:, 4 * LV:4 * LV + 8], AF.Silu)
    act(scbf[:, :, 1], colv[:, 4 * LV + 8:4 * LV + 16], AF.Silu)
    for s_ in strip:
        memset(s_[:, :, :], 0.0)

    class Path:
        pass

    paths = []
    for j, (xT, T) in enumerate(((xTs, TS), (xTp, TP))):
        p = Path()
        p.j, p.xT, p.T, p.NT, p.NTC = j, xT, T, T // 512, T // 128
        p.sample = (j == 0)
        p.segs = [(0, TS)] if j == 0 else [(0, 256), (256, 512)]
        paths.append(p)

    def tsl(t):
        return slice(t * 512, (t + 1) * 512)

    def rms_bcast(src_fn, n, all_act=False):
        ps = bank("mm")
        for k in range(n):
            s = rot(sqt, "sq")
            if all_act or k % 2 == 0:
                act(s[:, :], src_fn(k), AF.Square)
            else:
                tt(s[:, :], src_fn(k), src_fn(k), ALU.mult)
            mm(ps[:, :], ones_bf, s[:, :], k == 0, k == n - 1)
        act(rstd[:, :], ps[:, :], AF.Ln, bias=epsc)
        act(rstd[:, :], rstd[:, :], AF.Exp, scale=-0.5)

    def norm_mod(p, gs, sh):
        for t in range(p.NT):
            rms_bcast(lambda k: p.xT[:, k, tsl(t)], 8)
            for k in range(8):
                tf = rot(tmpf, "tf")
                tt(tf[:, :], p.xT[:, k, tsl(t)], rstd[:, :], ALU.mult)
                act(hT[:, k, tsl(t)], tf[:, :], AF.Identity, bias=sh[:, k, p.j:p.j + 1], scale=gs[:, k, p.j:p.j + 1])

    def post_norm(p, oT, t, gg):
        rms_bcast(lambda m: oT[:, m, :], 8, all_act=True)
        for m in range(8):
            tf = rot(tmpf, "tf")
            tt(tf[:, :], oT[:, m, :], rstd[:, :], ALU.mult, eng="pool" if m % 2 else "dve")
            stt(p.xT[:, m, tsl(t)], tf[:, :], gg[:, m, p.j:p.j + 1], p.xT[:, m, tsl(t)], ALU.mult, ALU.add)

    nrm_ctr = [0]

    def normalise(acc, out_ap, n=512):
        h0 = 64 * (nrm_ctr[0] % 2)
        nrm_ctr[0] += 1
        recip(rec[h0:h0 + 64, 0:n], acc[64:128, 0:n])
        tt(out_ap, acc[0:64, 0:n], rec[h0:h0 + 64, 0:n], ALU.mult)

    def stage_out(dst, src_ps, n=512, rows=128):
        s = rot(stg, "stg")
        copy(s[0:rows, 0:n], src_ps, eng="act")
        dma("sp", dst, s[0:rows, 0:n])

    def proj_fm(p, w, c0, consume):
        for t in range(p.NT):
            ps = bank("mm")
            for k in range(8):
                mm(ps[:, :], w[:, k, c0:c0 + 128], hT[:, k, tsl(t)], k == 0, k == 7)
            consume(t, ps)

    class Pipe:
        def __init__(self):
            self.pending = None

        def push(self, qk, rest):
            ctx = qk()
            if self.pending is not None:
                self.pending()
            self.pending = lambda: rest(ctx)

        def flush(self):
            if self.pending is not None:
                self.pending()
            self.pending = None

    def prompt_attn(pipe, p, kT_fn, qT_fn, v_fn, out_fn):
        accb = [None]
        for s in range(2):
            for kc in range(2):
                def qk(s=s, kc=kc):
                    pss = (bank("s4"), bank("s4"))
                    for i in range(2):
                        mm(pss[i][:, 0:256], kT_fn(i, slice(s * 256 + kc * 128, s * 256 + kc * 128 + 128)),
                           qT_fn(i, slice(s * 256, s * 256 + 256)))
                    return pss

                def rest(pss, s=s, kc=kc):
                    if accb[0] is None:
                        accb[0] = (bank("acc4"), bank("acc4"))
                    for i in range(2):
                        acc = accb[0][i]
                        pt = rot(ptl, "p")
                        act(pt[:, 0:256], pss[i][:, 0:256], AF.Exp, scale=0.125)
                        mm(acc[:, s * 256:(s + 1) * 256], v_fn(i, s * 2 + kc), pt[:, 0:256], kc == 0, kc == 1)
                    if s == 1 and kc == 1:
                        for i in range(2):
                            normalise(accb[0][i], out_fn(i, slice(0, 512)))
                pipe.push(qk, rest)

    def mods_load(l, sl):
        wada = w_ada_d[l].rearrange("(k p) n -> p k n", p=128)
        return load_w(wada[:, :, sl * 512:(sl + 1) * 512], 512), ctr["w"]

    def mods_compute(l, sl, ws):
        mods = mods2[l % 2]
        ps = bank("mm")
        for mq in range(4):
            for k in range(8):
                mm(ps[:, 2 * mq:2 * mq + 2], ws[:, k, mq * 128:(mq + 1) * 128], scbf[:, k, :], k == 0, k == 7)
        m0 = sl * 4
        tt(mods[:, m0:m0 + 4, :], ps[:, 0:8].rearrange("p (m j) -> p m j", j=2),
           cv(l, 32 + m0, 4).unsqueeze(2).broadcast_to([128, 4, 2]), ALU.add)

    def mods_slot(l, sl):
        ws, _ = mods_load(l, sl)
        mods_compute(l, sl, ws)

    mstate = {"l": None, "next": 12, "pend": None}

    def mnext(n=1):
        for _ in range(n):
            if mstate["l"] is None:
                return
            if mstate["pend"] is not None:
                sl, ws, at = mstate["pend"]
                assert ctr["w"] - at < len(wring), "adaLN slot evicted from the weight ring before use"
                mods_compute(mstate["l"], sl, ws)
                mstate["pend"] = None
            if mstate["next"] < 12:
                ws, at = mods_load(mstate["l"], mstate["next"])
                mstate["pend"] = (mstate["next"], ws, at)
                mstate["next"] += 1
            elif mstate["pend"] is None:
                return

    def mods_finish(l):
        mods, dsc, wg, nbf = mods2[l % 2], dsc2[l % 2], wg2[l % 2], nbf2[l % 2]
        win = w_in_d[l].rearrange("(k p) n -> p k n", p=128)
        gn = lambda i: cv(l, i * 8, 8).unsqueeze(2).broadcast_to([128, 8, 2])
        stt(dsc[:, 0, :, :], mods[:, 8:16, :], 1.0, gn(0), ALU.add, ALU.mult)
        tt(dsc[:, 1, :, :], mods[:, 16:24, :], gn(1), ALU.mult)
        stt(dsc[:, 2, :, :], mods[:, 32:40, :], 1.0, gn(2), ALU.add, ALU.mult)
        tt(dsc[:, 3, :, :], mods[:, 40:48, :], gn(3), ALU.mult)
        for dst0, src0 in ((0, 2560), (4, 2568), (8, 2564), (12, 2572)):
            dma("pool", wg[:, :, dst0:dst0 + 4], win[:, :, src0:src0 + 4])
        ts(nbf[:, :], cv(l, 85, 1)[0:8, :], -1.0, None, ALU.mult)

    def layers():
      for l in range(n_layers):
        win = w_in_d[l].rearrange("(k p) n -> p k n", p=128)
        wout = w_out_d[l].rearrange("(k p) n -> p k n", p=128)
        wgu = w_gu_d[l].rearrange("(k p) n -> p k n", p=128)
        wdown = w_down_d[l].rearrange("(k p) n -> p k n", p=128)
        wada = w_ada_d[l].rearrange("(k p) n -> p k n", p=128)

        mods, dsc, wg, nbf = mods2[l % 2], dsc2[l % 2], wg2[l % 2], nbf2[l % 2]
        if l == 0:
            for sl in range(12):
                mods_slot(0, sl)
            mods_finish(0)
        gs1, gg1, gs2, gg2 = dsc[:, 0, :, :], dsc[:, 1, :, :], dsc[:, 2, :, :], dsc[:, 3, :, :]
        sh1, sh2 = mods[:, 0:8, :], mods[:, 24:32, :]
        chk('mods')
        for p in paths:
            T, NT, NTC = p.T, p.NT, p.NTC
            if p.sample and l + 1 < n_layers:
                mstate["l"], mstate["next"], mstate["pend"] = l + 1, 0, None
            else:
                mstate["l"] = None
            sfx = '_s' if p.sample else '_p'
            norm_mod(p, gs1, sh1)
            chk('norm1' + sfx)
            if l == 0 and p.sample:
                debug("hT", hT[:, :, :])

            naq = sv(0, [128, 2, 1024])
            nak = sv(2048, [128, 2, 1280])
            nav = sv(4608, [128, 10, 4, 128])
            memset(nav[:, :, :, 64:128], 1.0)
            wsA = load_w(win[:, :, 0:512], 512)
            wsB = load_w(win[:, :, 512:768], 256)

            def build_strips(pr):
                for i in range(2):
                    h = 2 * pr + i
                    st = strip[i]
                    dma("sp", nabst[0:64, :, :], nabT_d[l, h])
                    dma("sp", nabst[64:128, :, :], nabT_d[l, h])
                    act(st[0:64, 3:18, :], nabst[0:64, :, :], AF.Exp)
                    act(st[64:128, 4:19, :], nabst[64:128, :, :], AF.Exp)
            if p.sample:
                build_strips(0)
            for c in range(4):
                dstt = naq if c < 2 else nak

                def cons(t, ps, c=c, dstt=dstt):
                    copy(dstt[:, c % 2, tsl(t)], ps[:, :])
                    if (not p.sample) and c >= 2:
                        stage_out(o_nak[l, c - 2, :, tsl(t)], ps[:, :])
                proj_fm(p, wsA, c * 128, cons)
            for tc in range(NTC):
                ps = bank("mm")
                for k in range(8):
                    mm(ps[:, 0:256], hT[:, k, tc * 128:(tc + 1) * 128], wsB[:, k, 0:256], k == 0, k == 7)
                copy(nav[:, tc, :, 0:64], ps[:, 0:256].rearrange("p (h d) -> p h d", h=4))
                if not p.sample:
                    stage_out(o_nav[l, tc * 128:(tc + 1) * 128, :], ps[:, 0:256], n=256)
            chk('naproj' + sfx)
            if p.sample:
                for c in range(2):
                    dma("pool", nak[:, c, 1024:1280], na_kT_d[l, c])
                for kc in range(2):
                    dma("pool", nav[:, 8 + kc, :, 0:64],
                        na_v_d[l][:, kc * 128:(kc + 1) * 128, :].rearrange("h s d -> s h d"))
                pipe = Pipe()
                for pr in range(2):
                    c = pr
                    pipe.flush()
                    if pr > 0:
                        build_strips(pr)
                    for qt in range(2):
                        accb = [None]
                        kl = [(j, True) for j in (range(0, 6) if qt == 0 else range(2, 8))] + [(8, False), (9, False)]
                        for it, (kc, w_) in enumerate(kl):
                            def qk(kc=kc, w_=w_, qt=qt, c=c):
                                pss = (bank("s4"), bank("s4"))
                                for i in range(2):
                                    hf = i * 64
                                    mm(pss[i][:, :], nak[hf:hf + 64, c, kc * 128:(kc + 1) * 128], naq[hf:hf + 64, c, tsl(qt)], True, not w_)
                                if w_:
                                    for i in range(2):
                                        mm(pss[i][:, :], kaug[:, kc * 128:(kc + 1) * 128], qaug[:, tsl(qt)], False, True)
                                return pss

                            def rest(pss, kc=kc, w_=w_, qt=qt, c=c, pr=pr, it=it, n=len(kl), accb=accb):
                                if accb[0] is None:
                                    accb[0] = (bank("acc4"), bank("acc4"))
                                for i in range(2):
                                    acc = accb[0][i]
                                    pt = rot(ptl, "p")
                                    act(pt[:, :], pss[i][:, :], AF.Exp, scale=0.125)
                                    if w_:
                                        b0 = 10 - 2 * kc + 8 * qt
                                        tt(pt[:, :].rearrange("p (a b) -> p a b", a=8),
                                           pt[:, :].rearrange("p (a b) -> p a b", a=8), strip[i][:, b0:b0 + 8, :], ALU.mult)
                                    mm(acc[:, :], nav[:, kc, 2 * pr + i, :], pt[:, :], it == 0, it == n - 1)
                                if it == n - 1:
                                    for i in range(2):
                                        normalise(accb[0][i], mixT[i * 64:i * 64 + 64, c, tsl(qt)])
                            pipe.push(qk, rest)
                pipe.flush()
            else:
                pipe = Pipe()
                for pr in range(2):
                    c = pr
                    prompt_attn(pipe, p, lambda i, sl_, c=c: nak[i * 64:i * 64 + 64, c, sl_],
                                lambda i, sl_, c=c: naq[i * 64:i * 64 + 64, c, sl_],
                                lambda i, kc, pr=pr: nav[:, kc, 2 * pr + i, :],
                                lambda i, sl_, c=c: mixT[i * 64:i * 64 + 64, c, sl_])
                pipe.flush()
            if l == 0 and p.sample:
                debug("mix_na", mixT[:, 0:2, :])

            chk('na' + sfx)
            gq = sv(0, [128, 4, 1024])
            gk = sv(4096, [128, 2, 1280])
            gv = sv(6656, [128, 10, 2, 128])
            memset(gv[:, :, :, 64:128], 1.0)
            wsC = load_w(win[:, :, 768:1280], 512)
            slotD = rot(wring, "w")
            wsD = slotD[:, 0:8 * 384].rearrange("p (k n) -> p k n", k=8)
            for d0, s0, n_ in ((0, 1280, 64), (64, 1280, 64), (128, 1344, 64), (192, 1344, 64), (256, 1408, 128)):
                dma("pool", wsD[:, :, d0:d0 + n_], win[:, :, s0:s0 + n_])
            units = [(c, t) for c in range(6) for t in range(NT)]
            st_ = {}

            def g_a(u):
                c, t = u
                w_ = wsC if c < 4 else wsD
                c0 = c * 128 if c < 4 else (c - 4) * 128
                ps = bank("mm")
                for k in range(8):
                    mm(ps[:, :], w_[:, k, c0:c0 + 128], hT[:, k, tsl(t)], k == 0, k == 7)
                st_[u] = {"ps": ps}

            def g_b(u):
                s_q = rot(sqt, "sq")
                act(s_q[:, :], st_[u]["ps"][:, :], AF.Square)
                st_[u]["sq"] = s_q

            def g_c(u):
                ps2 = bank("acc")
                mm(ps2[:, :], blk_bf, st_[u]["sq"][:, :])
                st_[u]["ps2"] = ps2

            def g_d(u):
                act(rstd[:, :], st_[u]["ps2"][:, :], AF.Ln, bias=epsc)
                act(rstd[:, :], rstd[:, :], AF.Exp, scale=-0.5)

            def g_e(u):
                c, t = u
                ps = st_[u]["ps"]
                gcol = cv(l, 80 if c < 4 else 81, 1)
                dstt = gq[:, c, :] if c < 4 else gk[:, c - 4, :]
                if p.sample:
                    q_ = rot(qn2, "qn")
                    stt(q_[:, :], ps[:, :], gcol, rstd[:, :], ALU.mult, ALU.mult)
                    st_[u]["q"] = q_
                else:
                    stt(dstt[:, tsl(t)], ps[:, :], gcol, rstd[:, :], ALU.mult, ALU.mult)
                    if c >= 4:
                        s2 = rot(stg, "stg")
                        stt(s2[:, :], ps[:, :], gcol, rstd[:, :], ALU.mult, ALU.mult)
                        dma("sp", o_gk[l, (c - 4) * 64:(c - 4) * 64 + 64, tsl(t)], s2[0:64, :])

            def g_f(u):
                if not p.sample:
                    return
                ps3 = bank("acc")
                mm(ps3[:, :], rmat_bf, st_[u]["q"][:, :])
                st_[u]["ps3"] = ps3

            def g_g(u):
                c, t = u
                if not p.sample:
                    return
                dstt = gq[:, c, :] if c < 4 else gk[:, c - 4, :]
                q_, ps3 = st_[u]["q"], st_[u]["ps3"]
                tt(tmpf[0][:, :], q_[:, :], COS[:, tsl(t)], ALU.mult)
                tt(tmpf[1][:, :], ps3[:, :], SIN[:, tsl(t)], ALU.mult)
                tt(dstt[:, tsl(t)], tmpf[0][:, :], tmpf[1][:, :], ALU.add)

            nu = len(units)
            U = lambda i: units[i] if 0 <= i < nu else None
            for i in range(nu + 2):
                if U(i):
                    g_a(U(i))
                if U(i - 1):
                    g_c(U(i - 1))
                if U(i - 2):
                    g_f(U(i - 2))
                if U(i):
                    g_b(U(i))
                if U(i - 1):
                    g_d(U(i - 1))
                    g_e(U(i - 1))
                if U(i - 2):
                    g_g(U(i - 2))
            mnext(2)
            for tc in range(NTC):
                ps = bank("mm")
                for k in range(8):
                    mm(ps[:, 0:128], hT[:, k, tc * 128:(tc + 1) * 128], wsD[:, k, 256:384], k == 0, k == 7)
                copy(gv[:, tc, :, 0:64], ps[:, 0:128].rearrange("p (h d) -> p h d", h=2))
                if not p.sample:
                    stage_out(o_gv[l, tc * 128:(tc + 1) * 128, :], ps[:, 0:128], n=128)
            if p.sample:
                for kv in range(2):
                    dma("pool", gk[:, kv, 1024:1280], gq_kT_d[l, kv])
                for kc in range(2):
                    dma("pool", gv[:, 8 + kc, :, 0:64],
                        gq_v_d[l][:, kc * 128:(kc + 1) * 128, :].rearrange("h s d -> s h d"))
                pipe = Pipe()
                for pr in range(4):
                    kv, c = pr // 2, pr
                    for qt in range(2):
                        accb = [None]
                        for kc in range(10):
                            def qk(kc=kc, qt=qt, kv=kv, c=c):
                                pss = (bank("s4"), bank("s4"))
                                for i in range(2):
                                    hf = i * 64
                                    mm(pss[i][:, :], gk[hf:hf + 64, kv, kc * 128:(kc + 1) * 128], gq[hf:hf + 64, c, tsl(qt)])
                                return pss

                            def rest(pss, kc=kc, qt=qt, kv=kv, c=c, accb=accb):
                                if accb[0] is None:
                                    accb[0] = (bank("acc4"), bank("acc4"))
                                for i in range(2):
                                    acc = accb[0][i]
                                    pt = rot(ptl, "p")
                                    act(pt[:, :], pss[i][:, :], AF.Exp, scale=0.125)
                                    mm(acc[:, :], gv[:, kc, kv, :], pt[:, :], kc == 0, kc == 9)
                                if kc == 9:
                                    for i in range(2):
                                        normalise(accb[0][i], mixT[i * 64:i * 64 + 64, 2 + c, tsl(qt)])
                            pipe.push(qk, rest)
                pipe.flush()
            else:
                pipe = Pipe()
                for pr in range(4):
                    kv, c = pr // 2, pr
                    prompt_attn(pipe, p, lambda i, sl_, kv=kv: gk[i * 64:i * 64 + 64, kv, sl_],
                                lambda i, sl_, c=c: gq[i * 64:i * 64 + 64, c, sl_],
                                lambda i, kc, kv=kv: gv[:, kc, kv, :],
                                lambda i, sl_, c=c: mixT[i * 64:i * 64 + 64, 2 + c, sl_])
                pipe.flush()
            if l == 0 and p.sample:
                debug("mix_gqa", mixT[:, 2:6, :])

            chk('gqa' + sfx)
            mlq = sv(0, [128, 2, 1024])
            mlk = sv(2048, [128, 2, 1024])
            mlkt = sv(4096, [128, 8, 256])
            mlv = sv(6144, [128, 8, 4, 66])
            Htok = sv(8256, [128, 8, 256], F32)
            g1 = sv(8256, [8, 1024], F32, rows=8)
            g2 = sv(8256 + 2048, [8, 1024], F32, rows=8)
            g3 = sv(12352, [8, 1024], F32, rows=8)
            sg = outT0[:, :, :].bitcast(BF16).rearrange("p a b -> p (a b)")[:, 0:2048].rearrange("p (a b) -> p a b", a=2)
            memset(mlv[:, :, :, 64:66], 1.0)
            wsE = load_w(win[:, :, 1536:2048], 512)
            wsF = load_w(win[:, :, 2048:2560], 512)
            for t in range(NT):
                psI = bank("mm")
                for k in range(8):
                    mm(psI[0:8, :], wg[:, k, 0:8], hT[:, k, tsl(t)], k == 0, k == 7)
                psF = bank("mm")
                for k in range(8):
                    mm(psF[0:8, :], wg[:, k, 8:16], hT[:, k, tsl(t)], k == 0, k == 7)
                act(g1[:, tsl(t)], psI[0:8, :], AF.Identity, bias=cv(l, 84, 1)[0:8, :])
                act(g2[:, tsl(t)], psF[0:8, :], AF.Exp, bias=nbf[:, :], scale=-1.0)
            act(g2[:, 0:T], g2[:, 0:T], AF.Ln, bias=1.0)
            defer_begin()
            m0c = cv(l, 86, 1)[0:8, :] if p.sample else zero8
            GS = lambda w_, a, b: gsm[:, w_, a:b]
            CM, GOF, GOB, GIF, GIB, GO, GI, CA, CNF, CNB, CN, TMP, MF = range(13)
            for si, (s0, s1) in enumerate(p.segs):
                sc_o, sc_i = g3[:, s0:s1], g2[:, s0:s1]
                P.op("dve", lambda e, sc_o=sc_o, sc_i=sc_i: e.tensor_tensor_scan(
                    out=sc_o, data0=sc_i, data1=sc_i, initial=0.0, op0=ALU.add, op1=ALU.max),
                    w=[sc_o], r=[sc_i])
                copy(tots[:, si:si + 1], g3[:, s1 - 1:s1])
                ts(g2[:, s0:s1], g2[:, s0:s1], tots[:, si:si + 1], beta, ALU.add, ALU.mult)
                stt(g3[:, s0:s1], g3[:, s0:s1], alpha, g2[:, s0:s1], ALU.mult, ALU.add)
            tt(g1[:, 0:T], g1[:, 0:T], g3[:, 0:T], ALU.subtract)
            cm_o, cm_i = GS(CM, 0, NTC), g1[:, 0:T].rearrange("p (c s) -> p c s", s=128)
            P.op("dve", lambda e, cm_o=cm_o, cm_i=cm_i: e.tensor_reduce(out=cm_o, in_=cm_i, axis=AX.X, op=ALU.max),
                 w=[cm_o], r=[cm_i])
            for si, (s0, s1) in enumerate(p.segs):
                a0, a1 = s0 // 128, s1 // 128
                tt(GS(GOF, a0, a0 + 1), GS(CM, a0, a0 + 1), m0c, ALU.max)
                for c in range(a0 + 1, a1):
                    tt(GS(GOF, c, c + 1), GS(CM, c, c + 1), GS(GOF, c - 1, c), ALU.max)
                tt(GS(GOB, a1 - 1, a1), GS(CM, a1 - 1, a1), m0c, ALU.max)
                for c in range(a1 - 2, a0 - 1, -1):
                    tt(GS(GOB, c, c + 1), GS(CM, c, c + 1), GS(GOB, c + 1, c + 2), ALU.max)
                copy(GS(GIF, a0, a0 + 1), m0c)
                copy(GS(GIF, a0 + 1, a1), GS(GOF, a0, a1 - 1))
                copy(GS(GIB, a1 - 1, a1), m0c)
                copy(GS(GIB, a0, a1 - 1), GS(GOB, a0 + 1, a1))
            blend = lambda o, f, b: (ts(GS(TMP, 0, NTC), GS(f, 0, NTC), phi, None, ALU.mult),
                                     stt(GS(o, 0, NTC), GS(b, 0, NTC), omphi, GS(TMP, 0, NTC), ALU.mult, ALU.add))
            blend(GO, GOF, GOB)
            blend(GI, GIF, GIB)
            tt(GS(CA, 0, NTC), GS(GI, 0, NTC), GS(GO, 0, NTC), ALU.subtract)
            act(GS(CA, 0, NTC), GS(CA, 0, NTC), AF.Exp)
            for si, (s0, s1) in enumerate(p.segs):
                a0, a1 = s0 // 128, s1 // 128
                copy(GS(CNF, a0, a1 - 1), GS(CA, a0 + 1, a1))
                memset(GS(CNF, a1 - 1, a1), 1.0)
                copy(GS(CNB, a0 + 1, a1), GS(CA, a0, a1 - 1))
                memset(GS(CNB, a0, a0 + 1), 1.0)
                if not p.sample:
                    tt(GS(MF, si, si + 1), GS(GOF, a1 - 1, a1), tots[:, si:si + 1], ALU.subtract)
                    dma("sp", o_m[l, si:si + 1, :].rearrange("a r -> r a"), GS(MF, si, si + 1))
            blend(CN, CNF, CNB)
            bc = lambda w_: GS(w_, 0, NTC).unsqueeze(2).broadcast_to([8, NTC, 128])
            v3 = lambda g: g[:, 0:T].rearrange("p (c s) -> p c s", s=128)
            tt(v3(g1), v3(g1), bc(GO), ALU.subtract)
            act(g1[:, 0:T], g1[:, 0:T], AF.Exp)
            tt(v3(g3), v3(g3), bc(GO), ALU.add)
            act(g3[:, 0:T], g3[:, 0:T], AF.Exp, scale=-1.0)
            stt(v3(g2), v3(g1), 0.125, bc(CN), ALU.mult, ALU.mult)
            gl = defer_end()
            for c in range(4):
                dstt = mlq if c < 2 else mlk
                proj_fm(p, wsE, c * 128, lambda t, ps, c=c, dstt=dstt: (copy(dstt[:, c % 2, tsl(t)], ps[:, :]), drip(gl, 3)))
            mnext(1)
            for tc in range(NTC):
                ps = bank("mm")
                for k in range(8):
                    mm(ps[:, 0:256], hT[:, k, tc * 128:(tc + 1) * 128], wsE[:, k, 256:512], k == 0, k == 7)
                copy(mlkt[:, tc, :], ps[:, 0:256])
                drip(gl, 3)
                ps = bank("mm")
                for k in range(8):
                    mm(ps[:, 0:256], hT[:, k, tc * 128:(tc + 1) * 128], wsF[:, k, 0:256], k == 0, k == 7)
                copy(mlv[:, tc, :, 0:64], ps[:, 0:256].rearrange("p (h d) -> p h d", h=4))
                drip(gl, 3)
            for c in range(2):
                proj_fm(p, wsF, 256 + c * 128, lambda t, ps, c=c: (act(sg[:, c, tsl(t)], ps[:, :], AF.Sigmoid), drip(gl, 3)))
            drip(gl, 10 ** 6)
            mnext(1)
            pst = bank("mm")
            for xi, g in enumerate((g1, g2, g3)):
                for tc in range(NTC):
                    col = (xi * 8 + tc) * 8
                    mm(pst[:, col:col + 8], g[:, tc * 128:(tc + 1) * 128], i8)
            copy(tokm[:, :, :, :], pst[:, 0:192].rearrange("p (x c r) -> p x c r", x=3, c=8))
            utok, wtok, fltok = tokm[:, 0, :, :], tokm[:, 1, :, :], tokm[:, 2, :, :]
            tt(rhsexp[:, :, 0:NTC], GS(CA, 0, NTC).unsqueeze(1).broadcast_to([8, 8, NTC]),
               i8.unsqueeze(2).broadcast_to([8, 8, NTC]), ALU.mult)
            psc = bank("mm")
            mm(psc[:, 0:64], ones8, rhsexp[:, :, :].rearrange("p a b -> p (a b)"))
            copy(carry_bc[:, :, :], psc[:, 0:64].rearrange("p (a b) -> p a b", a=8))
            if l == 0 and p.sample:
                debug("tokm", tokm[:, :, :, :])
                debug("carry_bc", carry_bc[:, :, :])

            chk('mlgate' + sfx)
            hwritten = set()
            for si, (s0, s1) in enumerate(p.segs):
                a0, a1 = s0 // 128, s1 // 128
                ncs = a1 - a0
                if p.sample:
                    mlc = ml_C_d[l].rearrange("(a two) k j -> two k a j", two=2)
                    c32v = C32[:, :, :].rearrange("p (a two) j -> p two a j", two=2)
                    for par in range(2):
                        dma("sp", c32v[par * 64:par * 64 + 64, par, :, :], mlc[par])
                    for r_ in range(8):
                        cf = a0 if r_ < 4 else a1 - 1
                        hs = slice((r_ % 2) * 64, (r_ % 2) * 64 + 64)
                        ts(C32[hs, r_, :], C32[hs, r_, :], carry_bc[hs, r_, cf:cf + 1], None, ALU.mult)
                        copy(C16[hs, r_, 0:65], C32[hs, r_, :], eng="act")
                else:
                    memset(C32[:, :, :], 0.0)
                    memset(C16[:, :, :], 0.0)
                chk('mlA' + sfx)
                its = [(step, d_) for step in range(ncs) for d_ in range(2)]
                cx = {}

                def ph_a(j):
                    step, d_ = its[j]
                    ch = a0 + step if d_ == 0 else a1 - 1 - step
                    psSb = [bank("s"), bank("s")]
                    psSv = lambda h: psSb[h % 2][:, (h // 2) * 128:(h // 2 + 1) * 128]
                    for h in range(4):
                        hf = (h % 2) * 64
                        mm(psSv(h), mlk[hf:hf + 64, h // 2, ch * 128:(ch + 1) * 128],
                           mlq[hf:hf + 64, h // 2, ch * 128:(ch + 1) * 128])
                    sms, uvs = [], []
                    for h in range(4):
                        r_ = d_ * 4 + h
                        sm = rot(smt, "sm")
                        stt(sm[:, :], psSv(h), utok[:, ch, r_:r_ + 1], maskd[d_], ALU.mult, ALU.mult)
                        uv = rot(uvt, "uv")
                        act(uv[:, 0:65], mlv[:, ch, h, 0:65], AF.Copy, scale=wtok[:, ch, r_:r_ + 1])
                        sms.append(sm)
                        uvs.append(uv)
                    cx[j] = (sms, uvs)

                def ph_b(j):
                    step, d_ = its[j]
                    ch = a0 + step if d_ == 0 else a1 - 1 - step
                    sms, uvs = cx[j]
                    psO = bank("acc")
                    psD = bank("mm")
                    for h in range(4):
                        hf = (h % 2) * 64
                        r_ = d_ * 4 + h
                        mm(psO[:, h * 66:h * 66 + 65], sms[h][:, :], mlv[:, ch, h, 0:65], True, False)
                        mm(psO[:, h * 66:h * 66 + 65], mlq[hf:hf + 64, h // 2, ch * 128:(ch + 1) * 128],
                           C16[hf:hf + 64, r_, 0:65], False, True)
                        mm(psD[0:64, h * 65:(h + 1) * 65], mlkt[:, ch, h * 64:(h + 1) * 64], uvs[h][:, 0:65])
                    cx[j] = (psO, psD)

                def ph_c(j):
                    step, d_ = its[j]
                    ch = a0 + step if d_ == 0 else a1 - 1 - step
                    nx = ch + 1 if d_ == 0 else ch - 1
                    last = step == ncs - 1
                    psO, psD = cx[j]
                    pso3 = psO[:, 0:264].rearrange("p (h j) -> p h j", h=4)
                    ts(den[:, :], pso3[:, :, 64], -1.0, None, ALU.mult)
                    stt(den[:, :], den[:, :], -1.0, den[:, :], ALU.mult, ALU.max)
                    tt(den[:, :], den[:, :], fltok[:, ch, d_ * 4:d_ * 4 + 4], ALU.max)
                    recip(den[:, :], den[:, :])
                    dH = Htok[:, ch, :].rearrange("p (h d) -> p h d", h=4)
                    dbc = den[:, :].unsqueeze(2).broadcast_to([128, 4, 64])
                    if (si, ch) not in hwritten:
                        hwritten.add((si, ch))
                        tt(dH, pso3[:, :, 0:64], dbc, ALU.mult)
                    else:
                        th = tmpH[:, :].rearrange("p (h d) -> p h d", h=4)
                        tt(th, pso3[:, :, 0:64], dbc, ALU.mult)
                        tt(dH, dH, th, ALU.add)
                    for h in range(4):
                        r_ = d_ * 4 + h
                        hs = slice((h % 2) * 64, (h % 2) * 64 + 64)
                        pd = psD[0:64, h * 65:(h + 1) * 65]
                        if not last:
                            stt(C32[hs, r_, :], C32[hs, r_, :], carry_bc[hs, r_, nx:nx + 1], pd, ALU.mult, ALU.add)
                            copy(C16[hs, r_, 0:65], C32[hs, r_, :], eng="act")
                        else:
                            tt(C32[hs, r_, :], C32[hs, r_, :], pd, ALU.add)

                ph_a(0)
                for j in range(len(its)):
                    if j + 1 < len(its):
                        ph_a(j + 1)
                    ph_b(j)
                    ph_c(j)
                    if j % 2 == 1:
                        mnext(1)
                if not p.sample:
                    ocv = o_C[l, si].rearrange("(a two) k j -> two k a j", two=2)
                    c32v = C32[:, :, :].rearrange("p (a two) j -> p two a j", two=2)
                    for par in range(2):
                        dma("sp", ocv[par], c32v[par * 64:par * 64 + 64, par, :, :])
            if l == 0 and p.sample:
                debug("Htok", Htok[:, :, :])
            chk('mlloop' + sfx)
            for tc in range(NTC):
                Hc = Htok[:, tc, :]
                tt(tmpH[:, :], Hc, Hc, ALU.mult)
                sq_o, sq_i = ssqa[:, tc, :], tmpH[:, :].rearrange("p (h d) -> p h d", h=4)
                P.op("dve", lambda e, sq_o=sq_o, sq_i=sq_i: e.tensor_reduce(out=sq_o, in_=sq_i, axis=AX.X, op=ALU.add),
                     w=[sq_o], r=[sq_i])
            act(ssqa[:, 0:NTC, :], ssqa[:, 0:NTC, :], AF.Ln, bias=epsc, scale=1.0 / 64)
            act(ssqa[:, 0:NTC, :], ssqa[:, 0:NTC, :], AF.Exp, scale=-0.5)
            for tc in range(NTC):
                Hc = Htok[:, tc, :]
                hnb = rot(hnb2, "hn")
                tt(hnb[:, :].rearrange("p (h d) -> p h d", h=4), Hc.rearrange("p (h d) -> p h d", h=4),
                   ssqa[:, tc, :].unsqueeze(2).broadcast_to([128, 4, 64]), ALU.mult)
                for c in range(2):
                    pT = bank("mm")
                    pTb = pT[:, :].bitcast(BF16)
                    transpose(pTb[:, 0:128], hnb[:, c * 128:(c + 1) * 128], ident_bf)
                    stt(mixT[:, 6 + c, tc * 128:(tc + 1) * 128], pTb[:, 0:128], cv(l, 82 + c, 1),
                        sg[:, c, tc * 128:(tc + 1) * 128], ALU.mult, ALU.mult)
            if l == 0 and p.sample:
                debug("mixT", mixT[:, :, :])

            chk('ml' + sfx)
            wo = [load_w(wout[:, :, i * 512:(i + 1) * 512], 512) for i in range(2)]
            for t in range(NT):
                oT = outT0 if t == 0 else outT1
                for m in range(8):
                    ps = bank("mm8")
                    for k in range(8):
                        mm(ps[:, :], wo[m // 4][:, k, (m % 4) * 128:(m % 4 + 1) * 128], mixT[:, k, tsl(t)], k == 0, k == 7)
                    copy(oT[:, m, :], ps[:, :], eng="act")
                post_norm(p, oT, t, gg1)
            if l == 0 and p.sample:
                debug("x_mid", xTs[:, :, :])

            chk('wout' + sfx)
            norm_mod(p, gs2, sh2)
            if mstate["l"] is not None:
                mnext(14)
                assert mstate["pend"] is None and mstate["next"] == 12
                mods_finish(mstate["l"])
                mstate["l"] = None
            for s_ in range(11):
                slot = rot(wring, "w")
                wv = slot[:, :].rearrange("p (k n) -> p k n", k=8)
                dma("pool", wv[:, :, 0:256], wgu[:, :, s_ * 256:(s_ + 1) * 256])
                dma("pool", wv[:, :, 256:512], wgu[:, :, FF + s_ * 256:FF + (s_ + 1) * 256])
                for jj in range(2):
                    j = s_ * 2 + jj
                    for t in range(NT):
                        psg = bank("mm")
                        for k in range(8):
                            mm(psg[:, :], wv[:, k, jj * 128:(jj + 1) * 128], hT[:, k, tsl(t)], k == 0, k == 7)
                        psu = bank("mm")
                        for k in range(8):
                            mm(psu[:, :], wv[:, k, 256 + jj * 128:256 + (jj + 1) * 128], hT[:, k, tsl(t)], k == 0, k == 7)
                        pt = rot(ptl, "p")
                        act(pt[:, :], psg[:, :], AF.Silu)
                        tt(hidden[:, j, tsl(t)], psu[:, :], pt[:, :], ALU.mult)
            for m in range(8):
                wd = load_w(wdown[:, :, m * 128:(m + 1) * 128], 128)
                for t in range(NT):
                    oT = outT0 if t == 0 else outT1
                    ps = bank("mm8")
                    for j in range(22):
                        mm(ps[:, :], wd[:, j, :], hidden[:, j, tsl(t)], j == 0, j == 21)
                    copy(oT[:, m, :], ps[:, :], eng="act" if m % 2 else "dve")
            for t in range(NT):
                post_norm(p, outT0 if t == 0 else outT1, t, gg2)
            chk('ffn' + sfx)
            if p.sample and l == n_layers - 1:
                for k in range(8):
                    dma("sp", ys_d[k * 128:(k + 1) * 128, :], xTs[:, k, :])

    try:
        layers()
    except _Stop:
        pass
    for k in range(8):
        if stop is not None:
            dma("sp", ys_d[k * 128:(k + 1) * 128, :], xTs[:, k, :])
        dma("sp", yp_d[k * 128:(k + 1) * 128, :], xTp[:, k, :])
    P.emit()
    es.close()
    return nc


def _consts():
    import ml_dtypes
    cf = np.zeros((128, NF), np.float32)
    t = np.arange(1024)
    row, col = (t // 64).astype(np.float64), (t % 64).astype(np.float64)
    inv = 1.0 / (10000.0 ** (np.arange(16, dtype=np.float64) / 16))
    for p in range(128):
        j = p % 64
        pos = row if j < 32 else col
        hd = j % 32
        ang = pos * inv[hd % 16]
        cf[p, CF_COS:CF_COS + 1024] = np.cos(ang)
        cf[p, CF_SIN:CF_SIN + 1024] = (-np.sin(ang) if hd < 16 else np.sin(ang))
    cf[:, CF_ID:CF_ID + 128] = np.eye(128)
    cf[0:4, CF_AL], cf[4:8, CF_AL] = -1.0, 1.0
    cf[0:4, CF_BE], cf[4:8, CF_BE] = 0.0, -1.0
    cf[0:4, CF_PHI], cf[4:8, CF_OMP] = 1.0, 1.0
    cf[0:8, CF_I8:CF_I8 + 8] = np.eye(8)
    cf[0:8, CF_ONE8:CF_ONE8 + 128] = 1.0
    cf[:, CF_EPS] = EPS
    cb = np.zeros((128, NB), np.float32)
    cb[:, CB_ID:CB_ID + 128] = np.eye(128)
    cb[:, CB_ONES:CB_ONES + 128] = 1.0 / 1024
    cb[0:64, CB_BLK:CB_BLK + 64] = 1.0 / 64
    cb[64:128, CB_BLK + 64:CB_BLK + 128] = 1.0 / 64
    for p in range(128):
        partner = p + 16 if (p % 32) < 16 else p - 16
        cb[partner, CB_R + p] = 1.0
    s = np.arange(128)
    cb[:, CB_MF:CB_MF + 128] = 0.125 * (s[:, None] <= s[None, :])
    cb[:, CB_MB:CB_MB + 128] = 0.125 * (s[:, None] >= s[None, :])
    rk = t // 64
    for r in range(16):
        cb[r, CB_KA:CB_KA + 1024] = (rk == r)
        r0 = np.clip(rk - 4, 0, 8)
        inwin = (r >= r0) & (r < r0 + 8)
        cb[r, CB_QA:CB_QA + 1024] = np.where(inwin, 0.0, NEG)
    return cf, cb.astype(ml_dtypes.bfloat16)


def _na_toeplitz(na_bias):
    cq = np.arange(64)
    c0 = np.clip(cq - 8, 0, 48)
    ck = np.arange(64)
    mask = (ck[None, :] >= c0[:, None]) & (ck[None, :] < c0[:, None] + 16)
    dc = np.clip(ck[None, :] - cq[:, None], -15, 15) + 15
    g = na_bias[:, :, ::-1, :][:, :, :, dc]
    g = np.where(mask[None, None, None], g, np.float32(NEG))
    return np.ascontiguousarray(g.transpose(0, 1, 4, 2, 3)).astype(np.float32)


_CACHE = {}


def kernel(x_prompt, x_sample, cache_na_kv, cache_gqa_kv, state_mlstm_C, state_mlstm_n, state_mlstm_m,
           c, c_ctx, w_in, b_gates, w_out, g_norm, g_qk, g_mlstm, na_bias, w_ada, b_ada, w_gu, w_down,
           _n_layers=DEPTH, _dbg=None, _stop=None, _cores=8):
    f = lambda a: np.ascontiguousarray(np.asarray(a, dtype=np.float32))
    x_prompt, x_sample = f(x_prompt), f(x_sample)
    cache_na_kv, cache_gqa_kv = f(cache_na_kv), f(cache_gqa_kv)
    sC, sn, sm = f(state_mlstm_C), f(state_mlstm_n), f(state_mlstm_m)
    c, c_ctx, b_gates, g_norm, g_qk, g_mlstm, na_bias, b_ada = map(f, (c, c_ctx, b_gates, g_norm, g_qk, g_mlstm, na_bias, b_ada))
    key = (_n_layers, tuple(_dbg or ()), _stop)
    if key not in _CACHE:
        _CACHE[key] = build(_n_layers, _dbg, _stop)
    nc = _CACHE[key]
    cf, cb = _consts()
    nabT = _na_toeplitz(na_bias)
    shared = dict(w_in=f(w_in), w_out=f(w_out), w_gu=f(w_gu), w_down=f(w_down), w_ada=f(w_ada),
                  cstf=cf, cstb=cb, nabT=nabT)
    fm = lambda v: v.reshape(-1, 128).T
    in_maps = []
    for b in range(_cores):
        cvv = np.zeros((128, NV), np.float32)
        for l in range(DEPTH):
            o = l * LV
            cvv[:, o:o + 32] = fm(g_norm[l].reshape(-1))
            cvv[:, o + 32:o + 80] = fm(b_ada[l])
            cvv[:, o + 80] = np.tile(g_qk[l, 0], 2)
            cvv[:, o + 81] = np.tile(g_qk[l, 1], 2)
            cvv[:, o + 82:o + 84] = fm(g_mlstm[l])
            cvv[0:8, o + 84] = np.concatenate([b_gates[l, 0:4], b_gates[l, 8:12]])
            cvv[0:8, o + 85] = np.concatenate([b_gates[l, 4:8], b_gates[l, 12:16]])
            cvv[0:8, o + 86] = sm[b, l].reshape(8)
        cvv[:, 4 * LV:4 * LV + 8] = fm(c[b])
        cvv[:, 4 * LV + 8:4 * LV + 16] = fm(c_ctx)
        na_kT = cache_na_kv[b, :, 0].transpose(0, 1, 3, 2).reshape(DEPTH, 2, 128, 256)
        gkT = cache_gqa_kv[b, :, 0].transpose(0, 1, 3, 2)
        gkT = np.concatenate([gkT, gkT], axis=2)
        mlC = np.concatenate([sC[b].transpose(0, 1, 2, 4, 3), sn[b][..., None]], axis=-1).reshape(DEPTH, 8, 64, 65)
        m = dict(shared)
        m.update(xs=np.ascontiguousarray(x_sample[b].T),
                 xp=np.ascontiguousarray(x_prompt[2 * b:2 * b + 2].reshape(TP, D).T),
                 colv=cvv, na_kT=np.ascontiguousarray(na_kT), na_v=np.ascontiguousarray(cache_na_kv[b, :, 1]),
                 gq_kT=np.ascontiguousarray(gkT), gq_v=np.ascontiguousarray(cache_gqa_kv[b, :, 1]),
                 ml_C=np.ascontiguousarray(mlC))
        in_maps.append(m)
    res = run_bass_kernel_spmd(nc, in_maps, core_ids=list(range(_cores)))
    R = res.results
    y_p = np.zeros((16, 256, D), np.float32)
    y_s = np.zeros((8, TS, D), np.float32)
    nna = np.zeros((16, DEPTH, 2, 4, 256, 64), np.float32)
    ngq = np.zeros((16, DEPTH, 2, 2, 256, 64), np.float32)
    nC = np.zeros((16, DEPTH, 2, 4, 64, 64), np.float32)
    nn = np.zeros((16, DEPTH, 2, 4, 64), np.float32)
    nm = np.zeros((16, DEPTH, 2, 4), np.float32)
    for b in range(_cores):
        r = R[b]
        y_s[b] = r["ys"].T
        y_p[2 * b:2 * b + 2] = r["yp"].T.reshape(2, 256, D)
        nak = r["o_nak"].reshape(DEPTH, 4, 64, 2, 256)
        nav = r["o_nav"].reshape(DEPTH, 2, 256, 4, 64)
        gkk = r["o_gk"].reshape(DEPTH, 2, 64, 2, 256)
        gvv = r["o_gv"].reshape(DEPTH, 2, 256, 2, 64)
        oC = r["o_C"].reshape(DEPTH, 2, 2, 4, 64, 65)
        om = r["o_m"].reshape(DEPTH, 2, 2, 4)
        for s in range(2):
            nna[2 * b + s, :, 0] = nak[:, :, :, s, :].transpose(0, 1, 3, 2)
            nna[2 * b + s, :, 1] = nav[:, s].transpose(0, 2, 1, 3)
            ngq[2 * b + s, :, 0] = gkk[:, :, :, s, :].transpose(0, 1, 3, 2)
            ngq[2 * b + s, :, 1] = gvv[:, s].transpose(0, 2, 1, 3)
            nC[2 * b + s] = oC[:, s, :, :, :, 0:64].transpose(0, 1, 2, 4, 3)
            nn[2 * b + s] = oC[:, s, :, :, :, 64]
            nm[2 * b + s] = om[:, s]
    kernel._last = R
    return (y_p, y_s, nna, ngq, nC, nn, nm)
```
